# Optimizing a Trainium2 kernel written in Bass

```python
import math
import jax
import jax.numpy as jnp
from jax import lax
import numpy as np

D_MODEL = 2048
BATCH = 8
SEQ = 4096
DEPTH = 4

GRID_W = 64
CTX_LEN = 256

BRANCH_W = D_MODEL // 2
N_BRANCH = 3
A_HEAD = 64
A_HEADS = BRANCH_W // A_HEAD
A_DECAY_RANK = 64
A_ICLR_RANK = 64
A_GATE_RANK = 128
A_CONV = 3
N_DIR = 2
LNX_EPS = 64e-5
B_WINDOWS = (2, 4, 8, 16)
B_GROUPS = len(B_WINDOWS)
B_GROUP_W = BRANCH_W // B_GROUPS
C_HEAD = 64
C_VHEAD = 2 * C_HEAD
C_HEADS = BRANCH_W // C_VHEAD
Q_BLOCK = 128
ROPE_BASE = 10000.0
D_FF = 256 * math.ceil(8 * D_MODEL / (3 * 256))
N_MOD = 6
SPLIT_SIZES = (3 * BRANCH_W, N_DIR * A_DECAY_RANK, N_DIR * A_ICLR_RANK, A_GATE_RANK, BRANCH_W, 2 * C_HEADS * C_HEAD, 2 * C_HEADS * C_HEAD, C_HEADS * C_VHEAD, N_BRANCH * D_MODEL)
D_IN = sum(SPLIT_SIZES)

kernel_name = 'hybrid_rwkv7_pool_diffattn_dit_block'

F32 = jnp.float32


def _rms(x, g, eps=1e-6):
    xf = x.astype(F32)
    y = xf * lax.rsqrt(jnp.mean(xf * xf, axis=-1, keepdims=True) + eps)
    return (y * g.astype(F32)).astype(x.dtype)


def _split_cols(z):
    idx = [int(i) for i in np.cumsum(SPLIT_SIZES)[:-1]]
    return jnp.split(z, idx, axis=-1)


def _centred_conv(u, w):
    half = A_CONV // 2
    n = u.shape[1]
    up = jnp.pad(u, ((0, 0), (half, half), (0, 0)))
    return sum(up[:, j:j + n] * w[j] for j in range(A_CONV))


def _head_l2norm(u):
    uh = u.astype(F32).reshape(*u.shape[:-1], A_HEADS, A_HEAD)
    uh = uh / jnp.maximum(jnp.linalg.norm(uh, axis=-1, keepdims=True), 1e-12)
    return uh.reshape(u.shape).astype(u.dtype)


def _rwkv_prepare(z_rkv, z_dec, z_iclr, conv_w, w0, w_up, a0, a_up, kk_scale, ka):
    r, k, v = jnp.split(_centred_conv(z_rkv, conv_w), 3, axis=-1)
    kk = _head_l2norm(k * kk_scale)
    zd = z_dec.reshape(*z_dec.shape[:-1], N_DIR, A_DECAY_RANK)
    za = z_iclr.reshape(*z_iclr.shape[:-1], N_DIR, A_ICLR_RANK)
    dirs = []
    for d in range(N_DIR):
        w_raw = (w0[d] + jnp.tanh(zd[..., d, :]) @ w_up[d]).astype(F32)
        decay = jnp.exp(-jnp.exp(-jax.nn.softplus(-w_raw) - 0.5))
        a = jax.nn.sigmoid(a0[d] + za[..., d, :] @ a_up[d])
        key_d = k * (1 + (a - 1) * ka)
        dirs.append((decay, key_d, kk * a))
    return r, v, kk, dirs


def _wkv_scan(r, w, k, v, a, b, state0, reverse):
    bsz, n = r.shape[:2]

    def tm(u):
        return jnp.moveaxis(u.astype(F32).reshape(bsz, n, A_HEADS, A_HEAD), 1, 0)

    def step(s, inp):
        rt, wt, kt, vt, at, bt = inp
        sa = jnp.einsum('bhvk,bhk->bhv', s, at)
        s = s * wt[:, :, None, :] + sa[..., None] * bt[:, :, None, :] + vt[..., None] * kt[:, :, None, :]
        return s, jnp.einsum('bhvk,bhk->bhv', s, rt)

    s_fin, ys = lax.scan(step, state0, (tm(r), tm(w), tm(k), tm(v), tm(a), tm(b)), reverse=reverse)
    return jnp.moveaxis(ys, 0, 1).reshape(bsz, n, BRANCH_W), s_fin


def _rwkv_output(y, r, v, keys, z_gate, r_k, lnx_w, lnx_b, gate_up):
    shp = r.shape

    def heads(u):
        return u.astype(F32).reshape(*u.shape[:-1], A_HEADS, A_HEAD)

    yh = heads(y)
    mu = jnp.mean(yh, axis=-1, keepdims=True)
    var = jnp.mean(jnp.square(yh - mu), axis=-1, keepdims=True)
    yh = (yh - mu) * lax.rsqrt(var + LNX_EPS) * heads(lnx_w) + heads(lnx_b)
    rh, vh, rkh = heads(r), heads(v), heads(r_k)
    bonus = sum(jnp.sum(rh * heads(kd) * rkh, axis=-1, keepdims=True) for kd in keys) * vh
    g = jax.nn.sigmoid(z_gate) @ gate_up
    return ((yh + bonus).reshape(shp) * g.astype(F32)).astype(v.dtype)


def _pool_mixer(u, pool_w, pool_scale):
    bsz, n, _ = u.shape
    ug = u.reshape(bsz, n, B_GROUPS, B_GROUP_W)
    csum = jnp.pad(jnp.cumsum(ug.astype(F32), axis=1), ((0, 0), (1, 0), (0, 0), (0, 0)))
    t = jnp.arange(n)
    outs = []
    for gi, win in enumerate(B_WINDOWS):
        lo = jnp.clip(t - win // 2, 0, n)
        hi = jnp.clip(t + win // 2, 0, n)
        cg = csum[:, :, gi]
        mean = (cg[:, hi] - cg[:, lo]) / (hi - lo).astype(F32)[None, :, None]
        outs.append(mean - ug[:, :, gi].astype(F32))
    pooled = jnp.stack(outs, axis=2).astype(u.dtype)
    mixed = jnp.einsum('btgc,gcd->btgd', pooled, pool_w)
    return mixed.reshape(bsz, n, BRANCH_W) * pool_scale


def _axial_rope_tables(rows):
    n_freq = C_HEAD // 4
    inv = ROPE_BASE ** (-jnp.arange(n_freq, dtype=F32) / n_freq)
    t_row = jnp.repeat(jnp.arange(rows, dtype=F32), GRID_W)
    t_col = jnp.tile(jnp.arange(GRID_W, dtype=F32), rows)
    ang = jnp.stack([t_row[:, None] * inv, t_col[:, None] * inv], axis=1)
    return jnp.cos(ang), jnp.sin(ang)


def _apply_rope(u, cos, sin):
    shp = u.shape
    uf = u.astype(F32).reshape(*shp[:-1], 2, 2, C_HEAD // 4)
    ua, ub = uf[..., 0, :], uf[..., 1, :]
    cs, sn = cos[None, :, None, None], sin[None, :, None, None]
    out = jnp.stack([ua * cs - ub * sn, ua * sn + ub * cs], axis=-2)
    return out.reshape(shp).astype(u.dtype)


def _heads_qk(z):
    return z.reshape(*z.shape[:-1], C_HEADS, 2, C_HEAD)


def _heads_v(z):
    return z.reshape(*z.shape[:-1], C_HEADS, C_VHEAD)


def _diff_lambda(lam_qk, lam_init):
    lq = lam_qk.astype(F32)
    return jnp.exp(jnp.sum(lq[0] * lq[1])) - jnp.exp(jnp.sum(lq[2] * lq[3])) + lam_init


def _diff_attention(q, k, v, lam, subln_g, lam_init):
    bsz, nq = q.shape[:2]
    nblk = nq // Q_BLOCK
    qb = jnp.moveaxis(q.reshape(bsz, nblk, Q_BLOCK, C_HEADS, 2, C_HEAD), 1, 0)
    scale = C_HEAD ** -0.5

    def one(qi):
        s = jnp.einsum('bqhjd,bkhjd->bhjqk', qi, k).astype(F32) * scale
        p = jax.nn.softmax(s, axis=-1)
        amap = p[:, :, 0] - lam * p[:, :, 1]
        return jnp.einsum('bhqk,bkhe->bqhe', amap.astype(v.dtype), v)

    o = jnp.moveaxis(lax.map(one, qb), 0, 1).reshape(bsz, nq, C_HEADS, C_VHEAD)
    o = _rms(o, subln_g, eps=1e-5) * (1 - lam_init)
    return o.reshape(bsz, nq, BRANCH_W)


def _merge(branches, z_mix, w_branch_l, w_out_l):
    gates = jax.nn.sigmoid(z_mix.reshape(*z_mix.shape[:-1], N_BRANCH, D_MODEL))
    acc = sum(gates[..., i, :] * (y @ w_branch_l[i]) for i, y in enumerate(branches))
    return acc @ w_out_l


def _swiglu(h, w_up, w_down):
    g, u = jnp.split(h @ w_up, 2, axis=-1)
    return (jax.nn.silu(g) * u) @ w_down


def setup_inputs(seed: int = 0) -> dict:
    key = jax.random.key(seed)
    ks = jax.random.split(key, 32)
    L = DEPTH

    def nrm(i, shape, scale):
        return jax.random.normal(ks[i], shape, F32) * scale

    def near_one(i, shape):
        return 1.0 + nrm(i, shape, 0.05)

    return {
        'x': nrm(0, (BATCH, SEQ, D_MODEL), 1.0),
        'c': nrm(1, (BATCH, D_MODEL), 1.0),
        'ctx': nrm(2, (BATCH, CTX_LEN, D_MODEL), 1.0),
        'c_ctx': nrm(3, (D_MODEL,), 1.0),
        'w_ada': nrm(4, (L, D_MODEL, N_MOD * D_MODEL), 0.5 * D_MODEL ** -0.5),
        'b_ada': nrm(5, (L, N_MOD * D_MODEL), 0.02),
        'norm_g': near_one(6, (L, 2, D_MODEL)),
        'w_in': nrm(7, (L, D_MODEL, D_IN), D_MODEL ** -0.5),
        'rkv_conv': jnp.array([0.25, 0.5, 0.25], F32)[None, :, None] + nrm(8, (L, A_CONV, 3 * BRANCH_W), 0.1),
        'decay_w0': jax.random.uniform(ks[9], (L, N_DIR, BRANCH_W), F32, -6.0, 1.0),
        'decay_up': nrm(10, (L, N_DIR, A_DECAY_RANK, BRANCH_W), 0.1),
        'iclr_a0': nrm(11, (L, N_DIR, BRANCH_W), 0.5),
        'iclr_up': nrm(12, (L, N_DIR, A_ICLR_RANK, BRANCH_W), 0.1),
        'gate_up': nrm(13, (L, A_GATE_RANK, BRANCH_W), A_GATE_RANK ** -0.5),
        'k_k': 0.85 + nrm(14, (L, BRANCH_W), 0.05),
        'k_a': near_one(15, (L, BRANCH_W)),
        'r_k': nrm(16, (L, BRANCH_W), 0.1),
        'lnx_w': near_one(17, (L, BRANCH_W)),
        'lnx_b': nrm(18, (L, BRANCH_W), 0.02),
        'pool_w': nrm(19, (L, B_GROUPS, B_GROUP_W, B_GROUP_W), B_GROUP_W ** -0.5),
        'pool_scale': near_one(20, (L, BRANCH_W)),
        'lam_qk': nrm(21, (L, 4, C_HEAD), 0.1),
        'subln_g': near_one(22, (L, C_VHEAD)),
        'w_branch': nrm(23, (L, N_BRANCH, BRANCH_W, D_MODEL), BRANCH_W ** -0.5),
        'w_out': nrm(24, (L, D_MODEL, D_MODEL), D_MODEL ** -0.5),
        'w_ffn_in': nrm(25, (L, D_MODEL, 2 * D_FF), D_MODEL ** -0.5),
        'w_ffn_out': nrm(26, (L, D_FF, D_MODEL), D_FF ** -0.5),
        'final_g': near_one(27, (D_MODEL,)),
    }


def reference(x, c, ctx, c_ctx, w_ada, b_ada, norm_g, w_in, rkv_conv, decay_w0, decay_up, iclr_a0, iclr_up, gate_up, k_k, k_a, r_k, lnx_w, lnx_b, pool_w, pool_scale, lam_qk, subln_g, w_branch, w_out, w_ffn_in, w_ffn_out, final_g):
    bsz, n_tok = x.shape[:2]
    rows = n_tok // GRID_W
    cos, sin = _axial_rope_tables(rows)
    silu_c = jax.nn.silu(c)[:, None, :]
    silu_cc = jax.nn.silu(c_ctx)
    xc = ctx
    for l in range(DEPTH):
        ctx_out = l < DEPTH - 1
        lam_init = 0.8 - 0.6 * math.exp(-0.3 * l)
        mod = jnp.split(silu_c @ w_ada[l] + b_ada[l], N_MOD, axis=-1)
        modc = jnp.split(silu_cc @ w_ada[l] + b_ada[l], N_MOD, axis=-1)

        hl = _rms(x, norm_g[l, 0]) * (1 + mod[1]) + mod[0]
        hc = _rms(xc, norm_g[l, 0]) * (1 + modc[1]) + modc[0]
        zl = _split_cols(hl @ w_in[l])
        zc = _split_cols(hc @ w_in[l])

        pa = (rkv_conv[l], decay_w0[l], decay_up[l], iclr_a0[l], iclr_up[l], k_k[l], k_a[l])
        ra_l, va_l, kk_l, dirs_l = _rwkv_prepare(zl[0], zl[1], zl[2], *pa)
        ra_c, va_c, kk_c, dirs_c = _rwkv_prepare(zc[0], zc[1], zc[2], *pa)
        s0 = jnp.zeros((bsz, A_HEADS, A_HEAD, A_HEAD), F32)
        ya_l = 0.0
        ya_c = 0.0
        for d in range(N_DIR):
            rev = d == 1
            dec_c, key_c, b_c = dirs_c[d]
            y_cd, s_ctx = _wkv_scan(ra_c, dec_c, key_c, va_c, -kk_c, b_c, s0, rev)
            dec_l, key_l, b_l = dirs_l[d]
            y_ld, _ = _wkv_scan(ra_l, dec_l, key_l, va_l, -kk_l, b_l, s_ctx, rev)
            ya_l = ya_l + y_ld
            if ctx_out:
                ya_c = ya_c + y_cd
        pout = (r_k[l], lnx_w[l], lnx_b[l], gate_up[l])
        ya_l = _rwkv_output(ya_l, ra_l, va_l, [dd[1] for dd in dirs_l], zl[3], *pout)

        yb_l = _pool_mixer(zl[4], pool_w[l], pool_scale[l])

        lam = _diff_lambda(lam_qk[l], lam_init)
        q_l = _apply_rope(_heads_qk(zl[5]), cos, sin)
        k_l = _apply_rope(_heads_qk(zl[6]), cos, sin)
        k_c = _heads_qk(zc[6])
        v_c = _heads_v(zc[7])
        k_all = jnp.concatenate([k_l, k_c], axis=1)
        v_all = jnp.concatenate([_heads_v(zl[7]), v_c], axis=1)
        yc_l = _diff_attention(q_l, k_all, v_all, lam, subln_g[l], lam_init)

        mix_l = _merge((ya_l, yb_l, yc_l), zl[8], w_branch[l], w_out[l])
        x = x + mod[2] * mix_l
        if ctx_out:
            ya_c = _rwkv_output(ya_c, ra_c, va_c, [dd[1] for dd in dirs_c], zc[3], *pout)
            yb_c = _pool_mixer(zc[4], pool_w[l], pool_scale[l])
            yc_c = _diff_attention(_heads_qk(zc[5]), k_c, v_c, lam, subln_g[l], lam_init)
            xc = xc + modc[2] * _merge((ya_c, yb_c, yc_c), zc[8], w_branch[l], w_out[l])

        h2 = _rms(x, norm_g[l, 1]) * (1 + mod[4]) + mod[3]
        x = x + mod[5] * _swiglu(h2, w_ffn_in[l], w_ffn_out[l])
        if ctx_out:
            h2c = _rms(xc, norm_g[l, 1]) * (1 + modc[4]) + modc[3]
            xc = xc + modc[5] * _swiglu(h2c, w_ffn_in[l], w_ffn_out[l])
    return _rms(x, final_g)
```

```python
import contextlib
import math
import numpy as np
import concourse.bass as bass
import concourse.mybir as mybir
from concourse.bass_utils import run_bass_kernel_spmd

F32 = mybir.dt.float32
BF16 = mybir.dt.bfloat16
AF = mybir.ActivationFunctionType
ALU = mybir.AluOpType
AX = mybir.AxisListType

D = 2048
SEQ = 4096
CTX = 256
T = SEQ + CTX
DEPTH = 4
D_IN = 13696
D_FF = 5632
BW = 1024
CH = 64
NCHUNK = T // CH
NSLAB_IN = 27

import os
VAR = os.environ.get('KVAR', '')
EPOCH = 60000
SAME_ENGINE_SYNC = True
NDS = {'sp': 40, 'pool': 24, 'act': 8}


def TT(out, in0, in1, op):
    return lambda e: e.tensor_tensor(out=out, in0=in0, in1=in1, op=op)


def TS(out, in0, s1, s2, op0, op1=None):
    if op1 is None:
        return lambda e: e.tensor_scalar(out=out, in0=in0, scalar1=s1, scalar2=s2, op0=op0)
    return lambda e: e.tensor_scalar(out=out, in0=in0, scalar1=s1, scalar2=s2, op0=op0, op1=op1)


def STT(out, in0, scalar, in1, op0, op1):
    return lambda e: e.scalar_tensor_tensor(out=out, in0=in0, scalar=scalar, in1=in1, op0=op0, op1=op1)


def ACT(out, in_, func, **kw):
    return lambda e: e.activation(out=out, in_=in_, func=func, **kw)


def CP(out, in_):
    return lambda e: e.tensor_copy(out=out, in_=in_)


def MM(out, lhsT, rhs, start=True, stop=True, tp=None):
    if tp is None:
        return lambda e: e.matmul(out, lhsT=lhsT, rhs=rhs, start=start, stop=stop)
    return lambda e: e.matmul(out, lhsT=lhsT, rhs=rhs, start=start, stop=stop, tile_position=tp)


class Tok:
    __slots__ = ('w', 'rs')

    def __init__(self):
        self.w = None
        self.rs = []


class K:
    ENG = ('pe', 'act', 'dve', 'pool', 'sp')

    def __init__(self, nc, stack):
        self.nc = nc
        self.stack = stack
        self.streams = {e: [] for e in self.ENG}
        self.cnt = {e: 0 for e in self.ENG}
        self.clock = {e: {f: 0 for f in self.ENG} for e in self.ENG}
        self.dma_done = {e: set() for e in self.ENG}
        self.sems = {}
        self.dq = {}
        for q, n in NDS.items():
            self.dq[q] = dict(sems=[stack.enter_context(nc.semaphore(f"dq_{q}{i}")) for i in range(n)],
                              val=[0] * n, last=[None] * n, n=0)
        self.dma_id = 0
        self.uid = 0
        self.rr = 0

    def sb(self, shape, dt, stack=None):
        self.uid += 1
        return (stack or self.stack).enter_context(self.nc.sbuf_tensor(f"t{self.uid}", list(shape), dt))

    def ps(self, shape, dt):
        self.uid += 1
        return self.stack.enter_context(self.nc.psum_tensor(f"p{self.uid}", list(shape), dt))

    def _sem(self, E, idx):
        key = (E, idx // EPOCH)
        if key not in self.sems:
            self.sems[key] = self.stack.enter_context(self.nc.semaphore(f"s_{E}_{key[1]}"))
        return self.sems[key], idx % EPOCH + 1

    def _wait(self, X, dep, force=False):
        if dep[0] == 'e':
            _, E, idx = dep
            if E == X and (X == 'pe' or not SAME_ENGINE_SYNC) and not force:
                return
            if self.clock[X][E] >= idx + 1:
                return
            self.clock[X][E] = idx + 1
            sem, val = self._sem(E, idx)
            self.streams[X].append(lambda e, sem=sem, val=val: e.wait_ge(sem, val))
        else:
            _, did, sem, target = dep
            if did in self.dma_done[X]:
                return
            self.dma_done[X].add(did)
            self.streams[X].append(lambda e, sem=sem, val=target: e.wait_ge(sem, val))

    def _deps(self, r, w):
        deps = []
        for t in r:
            if t.w is not None:
                deps.append(t.w)
        for t in w:
            if t.w is not None:
                deps.append(t.w)
            deps.extend(t.rs)
        return deps

    def _update(self, me, r, w):
        for t in r:
            t.rs.append(me)
            if len(t.rs) > 48:
                last = {}
                for d in t.rs:
                    key = d[1] if d[0] == 'e' else ('d', d[1])
                    if d[0] == 'e' and key in last and last[key][2] > d[2]:
                        continue
                    last[key] = d
                t.rs = list(last.values())
        for t in w:
            t.w = me
            t.rs = []

    def op(self, X, fn, r=(), w=()):
        for d in self._deps(r, w):
            self._wait(X, d)
        fns = fn if isinstance(fn, (list, tuple)) else [fn]
        idx = self.cnt[X]
        self.cnt[X] += 1
        sem, _ = self._sem(X, idx)
        st = self.streams[X]
        for f in fns[:-1]:
            st.append(f)
        st.append(lambda e, f=fns[-1], sem=sem: f(e).then_inc(sem, 1))
        me = ('e', X, idx)
        self._update(me, r, w)
        return me

    def alt(self, fa, fd, r=(), w=()):
        self.rr ^= 1
        if self.rr:
            return self.op('act', fa, r, w)
        return self.op('dve', fd, r, w)

    def dma(self, Q, out, in_, r=(), w=(), **kw):
        for d in self._deps(r, w):
            self._wait(Q, d)
        q = self.dq[Q]
        slot = q['n'] % len(q['sems'])
        q['n'] += 1
        if q['last'][slot] is not None:
            self._wait(Q, q['last'][slot])
        q['val'][slot] += 16
        target = q['val'][slot]
        sem = q['sems'][slot]
        did = self.dma_id
        self.dma_id += 1
        self.streams[Q].append(
            lambda e, out=out, in_=in_, sem=sem, kw=kw: e.dma_start(out=out, in_=in_, **kw).then_inc(sem, 16))
        me = ('d', did, sem, target)
        q['last'][slot] = me
        self._update(me, r, w)
        return me

    def barrier(self, queues=('sp', 'act'), engines=ENG):
        lasts = []
        for E in ('pe', 'act', 'dve', 'pool'):
            if self.cnt[E] > 0:
                lasts.append(('e', E, self.cnt[E] - 1))
        for Q in queues:
            for d in self.dq[Q]['last']:
                if d is not None:
                    lasts.append(d)
        for X in engines:
            for d in lasts:
                self._wait(X, d, force=True)

    def finish(self):
        self.barrier(queues=('sp', 'act', 'pool'))
        nc = self.nc
        st = self.streams
        with nc.Block() as block:
            @block.sync
            def _(e):
                for f in st['sp']:
                    f(e)

            @block.scalar
            def _(e):
                for f in st['act']:
                    f(e)

            @block.vector
            def _(e):
                for f in st['dve']:
                    f(e)

            @block.gpsimd
            def _(e):
                for f in st['pool']:
                    f(e)

            @block.tensor
            def _(e):
                for f in st['pe']:
                    f(e)


def host_consts():
    c = {}
    c['ident'] = np.eye(128, dtype=np.float32)
    i = np.arange(64)[:, None]
    t = np.arange(64)[None, :]
    masks = []
    for d in range(2):
        lt = (i < t) if d == 0 else (i > t)
        le = (i <= t) if d == 0 else (i >= t)
        m = np.concatenate([lt, le, lt, le, lt.T], axis=1).astype(np.float32)
        masks.append(np.tile(m, (2, 1)))
    c['masks'] = np.stack(masks, 1).astype(np.float32)
    hb = np.arange(128) // 64
    c['blkones'] = (hb[:, None] == hb[None, :]).astype(np.float32)
    c['ident2'] = np.tile(np.eye(64, dtype=np.float32), (2, 1))
    rs = np.ones((128, 1088), np.float32)
    rs[:, ::64] = 0.0
    c['scanreset'] = rs
    nf = 16
    inv = (10000.0 ** (-np.arange(nf, dtype=np.float32) / nf)).astype(np.float32)
    tt = np.arange(SEQ)
    t_row = (tt // 64).astype(np.float32)
    t_col = (tt % 64).astype(np.float32)
    ang = np.stack([t_row[:, None] * inv, t_col[:, None] * inv], axis=1).astype(np.float32)
    cos = np.cos(ang).astype(np.float32)
    sin = np.sin(ang).astype(np.float32)
    rows = np.arange(128)
    ax = (rows % 64) // 32
    f = rows % 16
    half = (rows % 32) // 16
    c['ropecos'] = np.ascontiguousarray(cos[:, ax, f].T)
    c['ropesin'] = np.ascontiguousarray(sin[:, ax, f].T)
    pm = np.zeros((128, 128), np.float32)
    for m in range(128):
        if half[m] == 0:
            pm[m + 16, m] = -1.0
        else:
            pm[m - 16, m] = 1.0
    c['ropeperm'] = pm
    pc = np.ones((128, 4, 2, 8), np.float32)
    for gi, w in enumerate((2, 4, 8, 16)):
        h = w // 2
        for tpos in range(8):
            if tpos < h:
                pc[:, gi, 0, tpos] = w / float(tpos + h)
            tr = 8 - tpos
            if h > tr:
                pc[:, gi, 1, tpos] = w / float(tr + h)
    c['poolcorr'] = pc
    return c


CST_LAYOUT = [('b_ada', 96), ('norm_g', 32), ('conv', 72), ('w0', 16), ('a0', 16), ('k_k', 8), ('k_a', 8),
              ('r_k', 8), ('pool_scale', 8)]
CST_OFF = {}
_o = 0
for _n, _s in CST_LAYOUT:
    CST_OFF[_n] = (_o, _s)
    _o += _s
NCST = _o


def fT(v, nch):
    return np.ascontiguousarray(np.asarray(v, np.float32).reshape(nch, 128).T)


def host_layer_consts(inp):
    L = DEPTH
    cst = np.zeros((L, 128, NCST), np.float32)
    lnx = np.zeros((L, 128, 8, 2, 64), np.float32)
    for l in range(L):
        def put(name, arr):
            o, s = CST_OFF[name]
            cst[l, :, o:o + s] = arr.reshape(128, s)
        put('b_ada', fT(inp['b_ada'][l], 96))
        put('norm_g', np.stack([fT(inp['norm_g'][l, i], 16) for i in range(2)], 1))
        put('conv', np.stack([fT(inp['rkv_conv'][l, j], 24) for j in range(3)], 1))
        put('w0', np.stack([fT(inp['decay_w0'][l, d], 8) for d in range(2)], 1))
        put('a0', np.stack([fT(inp['iclr_a0'][l, d], 8) for d in range(2)], 1))
        for n in ('k_k', 'k_a', 'r_k', 'pool_scale'):
            put(n, fT(inp[n][l], 8))
        for cc in range(8):
            for h in range(2):
                hd = 2 * cc + h
                lnx[l, h * 64:(h + 1) * 64, cc, 0, :] = inp['lnx_w'][l][hd * 64:(hd + 1) * 64][None, :]
                lnx[l, h * 64:(h + 1) * 64, cc, 1, :] = inp['lnx_b'][l][hd * 64:(hd + 1) * 64][None, :]
    sub = np.broadcast_to(np.asarray(inp['subln_g'], np.float32)[:, None, :], (L, 128, 128)).copy()
    lam = np.broadcast_to(np.asarray(inp['lam_qk'], np.float32).reshape(L, 1, 256), (L, 128, 256)).copy()
    return cst, lnx, sub, lam


class Prog:
    def __init__(self, nlayers=DEPTH, stages='all', debug=(), rw_stop=99, cg_stop=99):
        self.cg_stop = cg_stop
        self.rw_stop = rw_stop
        self.nl = nlayers
        self.stages = stages
        self.debug = set(debug)
        self.nc = bass.Bass("TRN2", target_bir_lowering=False)
        self.build()

    def din(self, name, shape, dt=F32):
        return self.nc.dram_tensor(name, list(shape), dt, kind="ExternalInput").ap()

    def dscr(self, name, shape, dt, out=False):
        kind = "ExternalOutput" if (out or name in self.debug) else "Internal"
        return self.nc.dram_tensor(name, list(shape), dt, kind=kind).ap()

    def build(self):
        nc = self.nc
        L = self.nl
        I = self.I = {}
        I['x'] = self.din('x', [SEQ, D])
        I['ctx'] = self.din('ctx', [CTX, D])
        I['cvec'] = self.din('cvec', [128, 16, 2])
        I['w_ada'] = self.din('w_ada', [L, D, 6 * D])
        I['w_in'] = self.din('w_in', [L, D, D_IN])
        I['w_branch'] = self.din('w_branch', [L, 3, BW, D])
        I['w_out'] = self.din('w_out', [L, D, D])
        I['w_ffn_in'] = self.din('w_ffn_in', [L, D, 2 * D_FF])
        I['w_ffn_out'] = self.din('w_ffn_out', [L, D_FF, D])
        I['pool_w'] = self.din('pool_w', [L, 4, 256, 256])
        I['decay_up'] = self.din('decay_up', [L, 128, BW])
        I['iclr_up'] = self.din('iclr_up', [L, 128, BW])
        I['gate_up'] = self.din('gate_up', [L, 128, BW])
        I['cst'] = self.din('cst', [L, 128, NCST])
        I['lnx'] = self.din('lnx', [L, 128, 8, 2, 64])
        I['subln'] = self.din('subln', [L, 128, 128])
        I['lamqk'] = self.din('lamqk', [L, 128, 256])
        I['final_g'] = self.din('final_g', [128, D])
        I['bada_bc'] = self.din('bada_bc', [L, 2, 128, D])
        for n, shp in (('ident', [128, 128]), ('masks', [128, 2, 320]), ('blkones', [128, 128]),
                       ('ident2', [128, 64]), ('scanreset', [128, 1088]), ('ropecos', [128, SEQ]),
                       ('ropesin', [128, SEQ]), ('ropeperm', [128, 128]), ('poolcorr', [128, 4, 2, 8])):
            I[n] = self.din(n, shp)
        self.out = nc.dram_tensor('out', [SEQ, D], F32, kind="ExternalOutput").ap()
        S = self.S = {}
        S['xres'] = self.dscr('xres', [T, D], F32)
        S['gbc'] = self.dscr('gbc', [2, 2, 128, D], F32)
        S['zT'] = self.dscr('zT', [D_IN, T], BF16)
        S['yT'] = self.dscr('yT', [3, BW, T], BF16)
        S['wb_in'] = self.dscr('wb_in', [L, NSLAB_IN, 128, 16, 512], BF16)
        S['wb_br'] = self.dscr('wb_br', [L, 3, 4, 128, 8, 512], BF16)
        S['wb_out'] = self.dscr('wb_out', [L, 4, 128, 16, 512], BF16)
        S['wb_f1'] = self.dscr('wb_f1', [L, 22, 128, 16, 512], BF16)
        S['wb_f2'] = self.dscr('wb_f2', [L, 8, 128, 44, 256], BF16)
        S['wb_pool'] = self.dscr('wb_pool', [L, 128, 4, 2, 256], BF16)
        S['wb_small'] = self.dscr('wb_small', [L, 3, 128, BW], BF16)
        with contextlib.ExitStack() as stack:
            self.k = k = K(nc, stack)
            self.stack = stack
            self.setup()
            for l in range(self.nl):
                self.layer(l)
            if self.stages == 'all':
                self.final()
            k.finish()

    def setup(self):
        k, I, S = self.k, self.I, self.S
        self.t_z = Tok()
        self.t_gbc = Tok()
        self.t_zr = [Tok() for _ in range(D_IN // 128)]
        self.t_y = [[Tok() for _ in range(8)] for _ in range(3)]
        self.bank = [k.ps([128, 512], F32) for _ in range(8)]
        self.bt = [Tok() for _ in range(8)]
        self.wtok = {}
        for l in range(self.nl):
            for s in range(NSLAB_IN):
                nc_ = 512 if s < 26 else 384
                tk = self.wtok[('in', l, s)] = Tok()
                k.dma('pool', S['wb_in'][l, s, :, :, 0:nc_],
                      I['w_in'][l][:, s * 512:s * 512 + nc_].rearrange("(kc p) n -> p kc n", p=128), w=[tk])
            tk = self.wtok[('small', l)] = Tok()
            for j, n in enumerate(('decay_up', 'iclr_up', 'gate_up')):
                k.dma('pool', S['wb_small'][l, j], I[n][l], w=[tk])
            tk = self.wtok[('pool', l)] = Tok()
            k.dma('pool', S['wb_pool'][l],
                  I['pool_w'][l].rearrange("g (cc p) d -> p g cc d", p=128), w=[tk])
            for i in range(3):
                for s in range(4):
                    tk = self.wtok[('br', l, i, s)] = Tok()
                    k.dma('pool', S['wb_br'][l, i, s],
                          I['w_branch'][l, i][:, s * 512:(s + 1) * 512].rearrange("(kc p) n -> p kc n", p=128),
                          w=[tk])
            for s in range(4):
                tk = self.wtok[('out', l, s)] = Tok()
                k.dma('pool', S['wb_out'][l, s],
                      I['w_out'][l][:, s * 512:(s + 1) * 512].rearrange("(kc p) n -> p kc n", p=128), w=[tk])
            for s in range(22):
                tk = self.wtok[('f1', l, s)] = Tok()
                k.dma('pool', S['wb_f1'][l, s],
                      I['w_ffn_in'][l][:, s * 512:(s + 1) * 512].rearrange("(kc p) n -> p kc n", p=128), w=[tk])
            for s in range(8):
                tk = self.wtok[('f2', l, s)] = Tok()
                k.dma('pool', S['wb_f2'][l, s],
                      I['w_ffn_out'][l][:, s * 256:(s + 1) * 256].rearrange("(kc p) n -> p kc n", p=128), w=[tk])
        self.t_x = Tok()
        k.dma('sp', S['xres'][0:CTX, :], I['ctx'], w=[self.t_x])
        k.dma('sp', S['xres'][CTX:T, :], I['x'], w=[self.t_x])
        self.C = C = {}
        self.t_c = Tok()
        idf = k.sb([128, 128], F32)
        C['identf'] = idf
        k.dma('sp', idf[:], I['ident'], w=[self.t_c])
        C['identb'] = k.sb([128, 128], BF16)
        k.op('dve', lambda e: e.tensor_copy(out=C['identb'][:], in_=idf[:]), r=[self.t_c], w=[self.t_c])
        cv = k.sb([128, 16, 2], F32)
        k.dma('sp', cv[:], I['cvec'], w=[self.t_c])
        C['sc'] = k.sb([128, 16, 2], F32)
        k.op('act', lambda e: e.activation(out=C['sc'][:], in_=cv[:], func=AF.Silu), r=[self.t_c], w=[self.t_c])
        C['cst'] = k.sb([128, NCST], F32)
        C['modT'] = k.sb([128, 6, 16, 2], F32)
        C['gsT'] = k.sb([128, 2, 16, 2], F32)
        self.t_cst = Tok()
        self.t_mod = Tok()

    def dbg(self, name, ap, toks, dt=F32):
        if 'dbg' not in self.debug:
            return
        shape = list(ap.shape)
        d = self.nc.dram_tensor('dbg_' + name, shape, dt, kind="ExternalOutput").ap()
        self.k.dma('sp', d, ap, r=list(toks))

    def cst(self, name):
        o, s = CST_OFF[name]
        return self.C['cst'][:, o:o + s]

    def layer(self, l):
        k = self.k
        k.dma('sp', self.C['cst'][:], self.I['cst'][l], w=[self.t_cst])
        self.phase_mod(l)
        self.phase_in(l)
        if self.stages == 'A':
            return
        ctx_out = l < DEPTH - 1
        if self.stages in ('all', 'B'):
            self.phase_conv(l)
            self.phase_rwkv(l)
        if self.stages == 'B':
            return
        if self.stages in ('all', 'C'):
            self.phase_pool(l)
        if self.stages == 'C':
            return
        if self.stages in ('all', 'D'):
            self.phase_attn(l, ctx_out)
        if self.stages == 'D':
            return
        self.phase_merge(l, ctx_out)
        if self.stages == 'E1':
            return
        self.phase_ffn(l, ctx_out)

    def phase_mod(self, l):
        k, C, I = self.k, self.C, self.I
        with contextlib.ExitStack() as st:
            wb = [k.sb([128, 16, 512], F32, st) for _ in range(2)]
            wt = [Tok() for _ in range(2)]
            pm = self.bank[0]
            sc_rep = k.sb([128, 2, 16, 128], F32, st)
            t_rep = Tok()
            for s_ in range(2):
                k.op('dve', CP(sc_rep[:, s_], C['sc'][:, :, s_:s_ + 1].to_broadcast([128, 16, 128])), r=[self.t_c], w=[t_rep])
            bbc = k.sb([128, 2, D], F32, st)
            k.dma('sp', bbc[:], I['bada_bc'][l].rearrange("j p d -> p j d"), w=[t_rep])
            gout = [k.sb([128, 512], F32, st) for _ in range(2)]
            t_gout = [Tok(), Tok()]
            ng = 0
            for slab in range(24):
                b = slab % 2
                k.dma('sp', wb[b][:],
                      I['w_ada'][l][:, slab * 512:(slab + 1) * 512].rearrange("(kc p) n -> p kc n", p=128),
                      w=[wt[b]])
                if slab // 4 in (2, 5):
                    jj = 0 if slab // 4 == 2 else 1
                    cs = slice((slab % 4) * 512, (slab % 4 + 1) * 512)
                    for s_ in range(2):
                        bi = 1 + ng % 4
                        gi_ = ng % 2
                        ng += 1
                        k.op('pe', [MM(self.bank[bi][:, 0:512], sc_rep[:, s_, kc, :], wb[b][:, kc, :], start=(kc == 0), stop=(kc == 15))
                                    for kc in range(16)], r=[wt[b], t_rep], w=[self.bt[bi]])
                        k.op('dve', TT(gout[gi_][:], self.bank[bi][:, 0:512], bbc[:, jj, cs], ALU.add), r=[self.bt[bi], t_rep], w=[t_gout[gi_]])
                        k.dma('sp', self.S['gbc'][jj, s_][:, cs], gout[gi_][:], r=[t_gout[gi_]], w=[self.t_gbc])
                for m in range(4):
                    ch = slab * 4 + m
                    k.op('pe', [lambda e, b=b, m=m, kc=kc, ch=ch: e.matmul(
                        pm[:, ch * 2:ch * 2 + 2], lhsT=wb[b][:, kc, m * 128:(m + 1) * 128], rhs=C['sc'][:, kc, :],
                        start=(kc == 0), stop=(kc == 15)) for kc in range(16)],
                        r=[wt[b], self.t_c], w=[self.bt[0]])
            o, s = CST_OFF['b_ada']
            bad = C['cst'][:, o:o + s]
            modv = C['modT'][:].rearrange("p j c s -> p (j c) s")
            k.op('dve', lambda e: e.tensor_tensor(out=modv, in0=pm[:, 0:192].rearrange("p (a s) -> p a s", s=2),
                                                  in1=bad.unsqueeze(2).to_broadcast([128, 96, 2]), op=ALU.add),
                 r=[self.bt[0], self.t_cst], w=[self.t_mod])
            og, _ = CST_OFF['norm_g']
            for i, j in ((0, 1), (1, 4)):
                g = C['cst'][:, og + i * 16:og + (i + 1) * 16]
                k.op('dve', lambda e, i=i, j=j, g=g: e.scalar_tensor_tensor(
                    out=C['gsT'][:, i], in0=C['modT'][:, j], scalar=1.0,
                    in1=g.unsqueeze(2).to_broadcast([128, 16, 2]), op0=ALU.add, op1=ALU.mult),
                    r=[self.t_mod, self.t_cst], w=[self.t_mod])
            k.barrier()

    def norm_transpose(self, xt, t_xt, nsub, which, s, hT, t_hT, scr):
        k, C = self.k, self.C
        ssq, rstd, junk, xn, t_s = scr['ssq'], scr['rstd'], scr['junk'], scr['xn'], scr['t']
        for sub in range(nsub):
            k.op('act', lambda e, sub=sub: e.activation(out=junk[:], in_=xt[:, sub, :], func=AF.Square,
                                                        accum_out=ssq[:, sub:sub + 1]),
                 r=[t_xt], w=[t_s['junk'], t_s['ssq']])
        k.op('dve', lambda e: e.tensor_scalar(out=rstd[:, 0:nsub], in0=ssq[:, 0:nsub], scalar1=1.0 / D, scalar2=1e-6,
                                              op0=ALU.mult, op1=ALU.add), r=[t_s['ssq']], w=[t_s['rstd']])
        k.op('dve', lambda e: e.reciprocal(out=rstd[:, 0:nsub], in_=rstd[:, 0:nsub]), r=[t_s['rstd']], w=[t_s['rstd']])
        k.op('act', lambda e: e.activation(out=rstd[:, 0:nsub], in_=rstd[:, 0:nsub], func=AF.Sqrt),
             r=[t_s['rstd']], w=[t_s['rstd']])
        for sub in range(nsub):
            k.op('act', lambda e, sub=sub: e.activation(out=xn[:, sub, :], in_=xt[:, sub, :], func=AF.Copy,
                                                        scale=rstd[:, sub:sub + 1]),
                 r=[t_xt, t_s['rstd']], w=[t_s['xn']])
        sh_j = 0 if which == 0 else 3
        for c in range(16):
            bi = 4 + (c % 2)
            pb = self.bank[bi][:].bitcast(BF16)
            k.op('pe', [lambda e, sub=sub, c=c, pb=pb: e.transpose(
                out=pb[:, sub * 128:(sub + 1) * 128], in_=xn[:, sub, c * 128:(c + 1) * 128],
                identity=C['identb'][:]) for sub in range(nsub)],
                r=[t_s['xn'], self.t_c], w=[self.bt[bi]])
            gs = C['gsT'][:, which, c, s:s + 1]
            sh = C['modT'][:, sh_j, c, s:s + 1]
            nt = nsub * 128
            k.alt(lambda e, c=c, pb=pb, gs=gs, sh=sh, nt=nt: e.activation(
                out=hT[:, c, 0:nt], in_=pb[:, 0:nt], func=AF.Identity, scale=gs, bias=sh),
                lambda e, c=c, pb=pb, gs=gs, sh=sh, nt=nt: e.tensor_scalar(
                    out=hT[:, c, 0:nt], in0=pb[:, 0:nt], scalar1=gs, scalar2=sh, op0=ALU.mult, op1=ALU.add),
                r=[self.bt[bi], self.t_mod], w=[t_hT])

    def norm_scratch(self, st, nsub=4):
        k = self.k
        return dict(ssq=k.sb([128, 4], F32, st), rstd=k.sb([128, 4], F32, st), junk=k.sb([128, D], BF16, st),
                    xn=k.sb([128, nsub, D], BF16, st),
                    t=dict(junk=Tok(), ssq=Tok(), rstd=Tok(), xn=Tok()))

    @staticmethod
    def tiles():
        res = [(0, CTX, 1)]
        for i in range(SEQ // 512):
            res.append((CTX + i * 512, 512, 0))
        return res

    def gemm_fm(self, slabs, KC, rhs, r_toks, NT, evac, wbufs, wtoks, banks=(0, 1, 2, 3), prefetch=None):
        k = self.k
        nb = len(wbufs)
        st = self._gemm_state

        def load(i):
            ap, ncols, tk = slabs[i]
            b = st['n'] % nb
            st['n'] += 1
            k.dma('sp', wbufs[b][:, 0:KC, 0:ncols], ap, r=[tk], w=[wtoks[b]])
            return b
        pend = [load(0)]
        m_glob = 0
        for i, (ap, ncols, tk) in enumerate(slabs):
            if i + 1 < len(slabs):
                pend.append(load(i + 1))
            b = pend.pop(0)
            for m in range(ncols // 128):
                bi = banks[st['bank'] % len(banks)]
                st['bank'] += 1
                ps = self.bank[bi]
                k.op('pe', [lambda e, b=b, m=m, kc=kc, ps=ps: e.matmul(
                    ps[:, 0:NT], lhsT=wbufs[b][:, kc, m * 128:(m + 1) * 128], rhs=rhs(kc),
                    start=(kc == 0), stop=(kc == KC - 1)) for kc in range(KC)],
                    r=[wtoks[b]] + list(r_toks), w=[self.bt[bi]])
                evac(m_glob, ps[:, 0:NT], self.bt[bi])
                m_glob += 1

    def phase_in(self, l):
        k, C, I, S = self.k, self.C, self.I, self.S
        self._gemm_state = dict(n=0, bank=0)
        with contextlib.ExitStack() as st:
            xt = [k.sb([128, 4, D], F32, st) for _ in range(2)]
            t_xt = [Tok() for _ in range(2)]
            hT = [k.sb([128, 16, 512], BF16, st) for _ in range(2)]
            t_hT = [Tok() for _ in range(2)]
            scr = self.norm_scratch(st)
            wbufs = [k.sb([128, 16, 512], BF16, st) for _ in range(3)]
            wtoks = [Tok() for _ in range(3)]
            zs = [k.sb([128, 4, 512], BF16, st) for _ in range(2)]
            t_zs = [Tok() for _ in range(2)]
            slabs = [(S['wb_in'][l, s, :, :, 0:(512 if s < 26 else 384)], 512 if s < 26 else 384,
                      self.wtok[('in', l, s)]) for s in range(NSLAB_IN)]
            tl = self.tiles()

            def load_x(i):
                t0, NT, s = tl[i]
                b = i % 2
                k.dma('sp', xt[b][:, 0:NT // 128, :],
                      S['xres'][t0:t0 + NT, :].rearrange("(s p) d -> p s d", p=128), r=[self.t_x], w=[t_xt[b]])
            load_x(0)
            zcount = [0]
            for i, (t0, NT, s) in enumerate(tl):
                b = i % 2
                if i + 1 < len(tl):
                    load_x(i + 1)
                self.norm_transpose(xt[b], t_xt[b], NT // 128, 0, s, hT[b], t_hT[b], scr)

                def evac(m, ps, btok, t0=t0, NT=NT):
                    zb = (zcount[0] // 4) % 2
                    mi = m % 4
                    k.alt(lambda e: e.activation(out=zs[zb][:, mi, 0:NT], in_=ps, func=AF.Copy),
                          lambda e: e.tensor_copy(out=zs[zb][:, mi, 0:NT], in_=ps),
                          r=[btok], w=[t_zs[zb]])
                    zcount[0] += 1
                    last = (m == D_IN // 128 - 1)
                    if mi == 3 or last:
                        nm = mi + 1
                        r0 = (m - mi) * 128
                        k.dma('sp', S['zT'][r0:r0 + nm * 128, t0:t0 + NT].rearrange("(m p) t -> p m t", p=128),
                              zs[zb][:, 0:nm, 0:NT], r=[t_zs[zb]], w=[self.t_zr[r0 // 128 + j_] for j_ in range(nm)])
                        if last:
                            zcount[0] = (zcount[0] + 3) // 4 * 4
                self.gemm_fm(slabs, 16, lambda kc, b=b, NT=NT: hT[b][:, kc, 0:NT], [t_hT[b]], NT, evac, wbufs, wtoks)
            k.barrier()

    def phase_conv(self, l):
        k, S, C = self.k, self.S, self.C
        oc, _ = CST_OFF['conv']
        with contextlib.ExitStack() as st:
            zin = [k.sb([128, T], BF16, st) for _ in range(2)]
            t_zin = [Tok(), Tok()]
            tmp = [k.sb([128, T], F32, st) for _ in range(2)]
            t_tmp = [Tok(), Tok()]
            zo = [k.sb([128, T], BF16, st) for _ in range(2)]
            t_zo = [Tok(), Tok()]
            segs = ((0, CTX), (CTX, T))
            for ch in range(24):
                b = ch % 2
                rows = S['zT'][ch * 128:(ch + 1) * 128, :]
                k.dma('sp', zin[b][:], rows, r=[self.t_zr[ch]], w=[t_zin[b]])
                w = [C['cst'][:, oc + j * 24 + ch:oc + j * 24 + ch + 1] for j in range(3)]
                k.op('dve', TS(tmp[b][:], zin[b][:], w[1], None, ALU.mult), r=[t_zin[b], self.t_cst], w=[t_tmp[b]])
                for (a, e_) in segs:
                    k.op('dve', STT(tmp[b][:, a + 1:e_], zin[b][:, a:e_ - 1], w[0], tmp[b][:, a + 1:e_], ALU.mult, ALU.add),
                         r=[t_zin[b]], w=[t_tmp[b]])
                for (a, e_) in segs:
                    k.op('dve', STT(tmp[b][:, a:e_ - 1], zin[b][:, a + 1:e_], w[2], tmp[b][:, a:e_ - 1], ALU.mult, ALU.add),
                         r=[t_zin[b]], w=[t_tmp[b]])
                k.op('act', ACT(zo[b][:], tmp[b][:], AF.Copy), r=[t_tmp[b]], w=[t_zo[b]])
                k.dma('sp', rows, zo[b][:], r=[t_zo[b]], w=[self.t_zr[ch]])
            k.barrier()

    def phase_rwkv(self, l):
        k, C, I, S = self.k, self.C, self.I, self.S
        NB, NCB = 256, 4
        NBLK = T // NB
        c0 = math.exp(-0.5)
        bank, bt = self.bank, self.bt
        identb, identf = C['identb'], C['identf']
        with contextlib.ExitStack() as st:
            t_lw = Tok()
            sgT = k.sb([128, T], BF16, st)
            wsm = k.sb([128, 3, BW], BF16, st)
            stage = k.sb([128, 128], F32, st)
            maskf = k.sb([128, 2, 320], F32, st)
            blk = k.sb([128, 128], BF16, st)
            id2 = k.sb([128, 64], F32, st)
            reset = k.sb([128, NB], F32, st)
            lnx = k.sb([128, 8, 2, 64], F32, st)
            onesf = k.sb([128, 2], F32, st)
            zerosf = k.sb([128, 256], F32, st)
            k.dma('sp', sgT[:], S['zT'][26 * 128:27 * 128, :], r=[self.t_zr[26]], w=[t_lw])
            k.op('act', ACT(sgT[:], sgT[:], AF.Sigmoid), r=[t_lw], w=[t_lw])
            k.dma('sp', wsm[:], S['wb_small'][l].rearrange("j p n -> p j n"), r=[self.wtok[('small', l)]], w=[t_lw])
            k.dma('sp', maskf[:], I['masks'], w=[t_lw])
            k.dma('sp', stage[:], I['blkones'], w=[t_lw])
            k.op('dve', CP(blk[:], stage[:]), r=[t_lw], w=[t_lw])
            k.dma('sp', id2[:], I['ident2'], w=[t_lw])
            k.dma('sp', reset[:], I['scanreset'][:, 0:NB], w=[t_lw])
            k.dma('sp', lnx[:], I['lnx'][l], w=[t_lw])
            k.op('dve', lambda e: e.memset(onesf[:], 1.0), w=[t_lw])
            k.op('dve', lambda e: e.memset(zerosf[:], 0.0), w=[t_lw])
            PcT = k.sb([128, NCHUNK, 64], F32, st)
            Gpp = k.sb([128, NCHUNK, 64], F32, st)
            RpT = k.sb([128, NCHUNK, 64], F32, st)
            Sall = k.sb([128, NCHUNK + 1, 64], F32, st)
            ysum = k.sb([128, NCHUNK, 64], F32, st)
            Vtok = k.sb([128, NCHUNK, 64], F32, st)
            bonus = k.sb([128, 2, NCHUNK], F32, st)
            WC = k.sb([128, NCHUNK], F32, st)
            yaT = k.sb([128, 512], BF16, st)
            t_P, t_G, t_R, t_ys, t_V, t_bo, t_WC, t_ya, t_S = (Tok() for _ in range(9))
            rb, kb, vb, thT, zaT, kq = (k.sb([128, NB], BF16, st) for _ in range(6))
            t_in = [Tok(), Tok(), Tok()]
            t_th, t_za, t_kq = Tok(), Tok(), Tok()
            k2, kka, rn, a_d, sg, ci_, ce_, wv, prod = (k.sb([128, NB], F32, st) for _ in range(9))
            t_k2, t_kka, t_rn, t_ad, t_sg, t_ci, t_ce, t_wv, t_prod = (Tok() for _ in range(9))
            key, t_key = rn, t_rn
            wi, t_wi = sg, t_sg
            arT = k.sb([128, NCB, 2, 64], F32, st)
            btT = k.sb([128, NB], F32, st)
            ktT = k.sb([128, NB], F32, st)
            t_ar, t_btT, t_ktT = Tok(), Tok(), Tok()
            lanes = []
            for ln in range(2):
                lanes.append(dict(
                    AM=k.sb([128, 2, 256], F32, st), A0BK=k.sb([128, 2, 192], F32, st),
                    NA=[k.sb([128, 2, 128], F32, st) for _ in range(2)],
                    Xb=k.sb([128, 2, 128], F32, st), X6=k.sb([128, 2, 128], F32, st),
                    t=dict(AM=Tok(), A0BK=Tok(), NA=[Tok(), Tok()], Xb=Tok(), X6=Tok()),
                    banks=[ln * 4 + i for i in range(4)],
                    pt=[Tok() for _ in range(4)]))
            yt = k.sb([128, 8, 64], F32, st)
            ysq = k.sb([128, 8, 64], F32, st)
            st8 = k.sb([128, 6, 8], F32, st)
            yf = k.sb([128, 8, 64], BF16, st)
            t_yt, t_ysq, t_st8, t_yf = Tok(), Tok(), Tok(), Tok()
            HP = (slice(0, 64), slice(64, 128))
            TP = ((0, 0), (64, 64))

            def chunk_group(ln, d, chs, lcs):
                L_ = lanes[ln]
                G = len(chs)
                bA, bB, bC, bD = (bank[b_] for b_ in L_['banks'])
                tA, tB, tX, tD = (bt[b_] for b_ in L_['banks'])
                tN = tD
                AM, A0BK, NA, Xb, X6 = L_['AM'], L_['A0BK'], L_['NA'], L_['Xb'], L_['X6']
                tt = L_['t']
                fns = []
                for ci, lc in enumerate(lcs):
                    cs = slice(lc * 64, (lc + 1) * 64)
                    for h in range(2):
                        hp, tp = HP[h], TP[h]
                        ar2 = arT[hp, lc, :, :].rearrange("p a t -> p (a t)")
                        idh = identf[hp, h * 64:(h + 1) * 64]
                        fns.append(MM(bA[hp, ci * 256:ci * 256 + 128], btT[hp, cs], ar2, tp=tp))
                        fns.append(MM(bA[hp, ci * 256 + 128:ci * 256 + 256], ktT[hp, cs], ar2, tp=tp))
                        fns.append(MM(bB[hp, ci * 192:ci * 192 + 64], arT[hp, lc, 0, :], btT[hp, cs], tp=tp))
                        fns.append(MM(bB[hp, ci * 192 + 64:ci * 192 + 128], btT[hp, cs], idh, tp=tp))
                        fns.append(MM(bB[hp, ci * 192 + 128:ci * 192 + 192], ktT[hp, cs], idh, tp=tp))
                k.op('pe', fns, r=[t_ar, t_btT, t_ktT, self.t_c], w=[tA, tB])
                bAv = bA[:, 0:G * 256].rearrange("p (c n) -> p c n", n=256)
                bBv = bB[:, 0:G * 192].rearrange("p (c n) -> p c n", n=192)
                k.op('dve', TT(AM[:, 0:G, :], bAv, maskf[:, d, 0:256].unsqueeze(1).to_broadcast([128, G, 256]), ALU.mult),
                     r=[tA, t_lw], w=[tt['AM']])
                k.op('act', ACT(A0BK[:, 0:G, 64:192], bBv[:, :, 64:192], AF.Copy), r=[tB], w=[tt['A0BK']])
                k.op('act', ACT(A0BK[:, 0:G, 0:64], bBv[:, :, 0:64], AF.Copy), r=[tB], w=[tt['A0BK']])
                k.op('dve', TT(A0BK[:, 0:G, 0:64], A0BK[:, 0:G, 0:64],
                               maskf[:, d, 256:320].unsqueeze(1).to_broadcast([128, G, 64]), ALU.mult),
                     r=[tt['A0BK'], t_lw], w=[tt['A0BK']])
                yield
                fns = [MM(bC[:, 0:G * 128], zerosf[:, 0:128], zerosf[:, 0:G * 128], start=True, stop=True)]
                for ci, (ch, lc) in enumerate(zip(chs, lcs)):
                    for h in range(2):
                        hp, tp = HP[h], TP[h]
                        fns.append(MM(bC[hp, ci * 128:ci * 128 + 64], arT[hp, lc, 0, :], identf[hp, h * 64:(h + 1) * 64],
                                      start=False, stop=True, tp=tp))
                        fns.append(MM(bC[hp, ci * 128 + 64:ci * 128 + 128], AM[hp, ci, 128:192], Vtok[hp, ch, :],
                                      start=False, stop=True, tp=tp))
                k.op('pe', fns, r=[t_ar, tt['AM'], t_V, self.t_c, t_lw], w=[tX])
                bXv = bC[:, 0:G * 128].rearrange("p (c n) -> p c n", n=128)
                bNv = bD[:, 0:G * 128].rearrange("p (c n) -> p c n", n=128)
                k.op('act', ACT(Xb[:, 0:G, :], bXv, AF.Copy), r=[tX], w=[tt['Xb']])
                yield
                for j in range(6):
                    fns = []
                    rd = [tt['Xb']]
                    for ci in range(G):
                        for h in range(2):
                            hp, tp = HP[h], TP[h]
                            if j == 0:
                                Nj, Aj = AM[hp, ci, 0:64], A0BK[hp, ci, 0:64]
                            else:
                                Nj, Aj = NA[j % 2][hp, ci, 0:64], NA[j % 2][hp, ci, 64:128]
                            fns.append(MM(bC[hp, ci * 128:ci * 128 + 128], Nj, Xb[hp, ci, :], start=False, stop=True, tp=tp))
                            if j < 5:
                                fns.append(MM(bD[hp, ci * 128:ci * 128 + 64], Aj, Nj, tp=tp))
                                fns.append(MM(bD[hp, ci * 128 + 64:ci * 128 + 128], Nj, Aj, tp=tp))
                    if j == 0:
                        rd += [tt['AM'], tt['A0BK']]
                    else:
                        rd += [tt['NA'][j % 2]]
                    k.op('pe', fns, r=rd, w=[tX, tN] if j < 5 else [tX])
                    if j < 5:
                        k.op('act', ACT(Xb[:, 0:G, :], bXv, AF.Copy), r=[tX], w=[tt['Xb']])
                        k.op('dve', TS(NA[(j + 1) % 2][:, 0:G, :], bNv, 1.0, None, ALU.mult), r=[tN], w=[tt['NA'][(j + 1) % 2]])
                    else:
                        k.op('act', ACT(X6[:, 0:G, :], bXv, AF.Copy), r=[tX], w=[tt['X6']])
                    yield
                fns = []
                for ci, (ch, lc) in enumerate(zip(chs, lcs)):
                    for h in range(2):
                        hp, tp = HP[h], TP[h]
                        o = ci * 256
                        Ap, U0 = X6[hp, ci, 0:64], X6[hp, ci, 64:128]
                        Btok, Ktok = A0BK[hp, ci, 64:128], A0BK[hp, ci, 128:192]
                        rbT, rkT = AM[hp, ci, 64:128], AM[hp, ci, 192:256]
                        V = Vtok[hp, ch, :]
                        fns.append(MM(bD[hp, o:o + 64], Ap, Btok, tp=tp))
                        fns.append(MM(bD[hp, o + 64:o + 128], Btok, U0, start=True, stop=False, tp=tp))
                        fns.append(MM(bD[hp, o + 64:o + 128], Ktok, V, start=False, stop=True, tp=tp))
                        fns.append(MM(bD[hp, o + 128:o + 192], Ap, rbT, start=True, stop=False, tp=tp))
                        fns.append(MM(bD[hp, o + 128:o + 192], identf[hp, h * 64:(h + 1) * 64], arT[hp, lc, 1, :],
                                      start=False, stop=True, tp=tp))
                        fns.append(MM(bD[hp, o + 192:o + 256], rbT, U0, start=True, stop=False, tp=tp))
                        fns.append(MM(bD[hp, o + 192:o + 256], rkT, V, start=False, stop=True, tp=tp))
                k.op('pe', fns, r=[tt['X6'], tt['A0BK'], tt['AM'], t_V, t_ar, self.t_c], w=[tD])
                bDv = bD[:, 0:G * 256].rearrange("p (c n) -> p c n", n=256)
                c_a, c_b = chs[0], chs[-1] + 1
                k.op('dve', TT(PcT[:, c_a:c_b, :], bDv[:, :, 0:64], id2[:].unsqueeze(1).to_broadcast([128, G, 64]), ALU.add),
                     r=[tD, t_lw], w=[t_P])
                k.op('dve', TT(Gpp[:, c_a:c_b, :], bDv[:, :, 64:128],
                               WC[:, c_a:c_b].unsqueeze(2).to_broadcast([128, G, 64]), ALU.mult),
                     r=[tD, t_WC], w=[t_G])
                k.op('dve', TS(RpT[:, c_a:c_b, :], bDv[:, :, 128:192], 1.0, None, ALU.mult), r=[tD], w=[t_R])
                k.op('dve', TT(ysum[:, c_a:c_b, :], ysum[:, c_a:c_b, :], bDv[:, :, 192:256], ALU.add), r=[tD, t_ys], w=[t_ys])
                yield

            def run_lanes(gens):
                act = list(gens)
                while act:
                    for g in list(act):
                        try:
                            next(g)
                        except StopIteration:
                            act.remove(g)

            og_w0, _ = CST_OFF['w0']
            og_a0, _ = CST_OFF['a0']
            order = [list(range(NCHUNK)), [3, 2, 1, 0] + list(range(NCHUNK - 1, 3, -1))]
            ncc = 8 if self.rw_stop > 5 else 1
            for cc in range(ncc):
                ccs = slice(cc * 128, (cc + 1) * 128)
                k.op('dve', lambda e: e.memset(ysum[:], 0.0), r=[t_ys], w=[t_ys])
                for d in range(2):
                    dp = slice(d * 64, (d + 1) * 64)
                    a0 = C['cst'][:, og_a0 + d * 8 + cc:og_a0 + d * 8 + cc + 1]
                    w0 = C['cst'][:, og_w0 + d * 8 + cc:og_w0 + d * 8 + cc + 1]
                    for bk in range(NBLK):
                        tk0 = bk * NB
                        c0_ = bk * NCB
                        tsl = slice(tk0, tk0 + NB)
                        for j, dst in enumerate((rb, kb, vb)):
                            if j == 2 and d == 1:
                                continue
                            chn = j * 8 + cc
                            k.dma('sp', dst[:], S['zT'][chn * 128:(chn + 1) * 128, tsl], r=[self.t_zr[chn]], w=[t_in[j]])
                        k.dma('sp', thT[:], S['zT'][24 * 128:25 * 128, tsl], r=[self.t_zr[24]], w=[t_th])
                        k.op('act', ACT(thT[:], thT[:], AF.Tanh), r=[t_th], w=[t_th])
                        k.dma('sp', zaT[:], S['zT'][25 * 128:26 * 128, tsl], r=[self.t_zr[25]], w=[t_za])
                        k.op('dve', TS(k2[:], kb[:], self.cst('k_k')[:, cc:cc + 1], None, ALU.mult), r=[t_in[1], self.t_cst], w=[t_k2])
                        k.op('act', ACT(kq[:], k2[:], AF.Square), r=[t_k2], w=[t_kq])
                        k.op('pe', MM(bank[0][:, 0:NB], blk[:], kq[:]), r=[t_kq, t_lw], w=[bt[0]])
                        k.op('dve', TS(rn[:], bank[0][:, 0:NB], 1e-24, None, ALU.max), r=[bt[0]], w=[t_rn])
                        k.op('dve', lambda e: e.reciprocal(out=rn[:], in_=rn[:]), r=[t_rn], w=[t_rn])
                        k.op('act', ACT(rn[:], rn[:], AF.Sqrt), r=[t_rn], w=[t_rn])
                        k.op('dve', TT(k2[:], k2[:], rn[:], ALU.mult), r=[t_k2, t_rn], w=[t_k2])
                        k.op('dve', TS(kka[:], kb[:], self.cst('k_a')[:, cc:cc + 1], None, ALU.mult), r=[t_in[1], self.t_cst], w=[t_kka])
                        if d == 0:
                            fns = []
                            for lc in range(NCB):
                                for h in range(2):
                                    fns.append(MM(bank[1][HP[h], lc * 64:(lc + 1) * 64], vb[HP[h], lc * 64:(lc + 1) * 64],
                                                  identb[HP[h], h * 64:(h + 1) * 64], tp=TP[h]))
                            k.op('pe', fns, r=[t_in[2], self.t_c], w=[bt[1]])
                            k.op('act', ACT(Vtok[:, c0_:c0_ + NCB, :],
                                            bank[1][:, 0:NCB * 64].rearrange("p (c v) -> p c v", v=64), AF.Copy),
                                 r=[bt[1]], w=[t_V])
                        k.op('pe', MM(bank[2][:, 0:NB], wsm[dp, 1, ccs], zaT[dp, :], tp=(64 * d, 0)), r=[t_lw, t_za], w=[bt[2]])
                        k.op('act', ACT(a_d[:], bank[2][:, 0:NB], AF.Sigmoid, bias=a0), r=[bt[2], self.t_cst], w=[t_ad])
                        k.op('pe', MM(bank[3][:, 0:NB], wsm[dp, 0, ccs], thT[dp, :], tp=(64 * d, 0)), r=[t_lw, t_th], w=[bt[3]])
                        k.op('act', ACT(sg[:], bank[3][:, 0:NB], AF.Sigmoid, bias=w0), r=[bt[3], self.t_cst], w=[t_sg])
                        k.op('dve', lambda e: e.tensor_tensor_scan(out=ci_[:], data0=reset[:], data1=sg[:], initial=0.0,
                                                                  op0=ALU.mult, op1=ALU.add), r=[t_sg, t_lw], w=[t_ci])
                        civ = ci_[:].rearrange("p (c t) -> p c t", t=64)
                        cev = ce_[:].rearrange("p (c t) -> p c t", t=64)
                        if d == 0:
                            k.op('dve', TT(ce_[:], ci_[:], sg[:], ALU.subtract), r=[t_ci, t_sg], w=[t_ce])
                        else:
                            k.op('dve', TT(cev, civ[:, :, 63:64].to_broadcast([128, NCB, 64]), civ, ALU.subtract),
                                 r=[t_ci], w=[t_ce])
                            k.op('dve', TT(ci_[:], ce_[:], sg[:], ALU.add), r=[t_ce, t_sg, t_ci], w=[t_ci])
                        k.op('dve', STT(key[:], a_d[:], -1.0, kka[:], ALU.add, ALU.mult), r=[t_ad, t_kka], w=[t_key])
                        k.op('dve', TT(key[:], key[:], kb[:], ALU.add), r=[t_key, t_in[1]], w=[t_key])
                        k.op('dve', TT(a_d[:], a_d[:], k2[:], ALU.mult), r=[t_ad, t_k2], w=[t_ad])
                        k.op('act', ACT(wi[:], ci_[:], AF.Exp, scale=-c0), r=[t_ci], w=[t_wi])
                        k.op('act', ACT(wv[:], ci_[:], AF.Exp, scale=c0), r=[t_ci], w=[t_wv])
                        k.op('act', ACT(ce_[:], ce_[:], AF.Exp, scale=-c0), r=[t_ce], w=[t_ce])
                        wiv = wi[:].rearrange("p (c t) -> p c t", t=64)
                        end = 63 if d == 0 else 0
                        k.op('dve', CP(WC[:, c0_:c0_ + NCB], wiv[:, :, end]), r=[t_wi], w=[t_WC])
                        k.op('dve', STT(arT[:, :, 0, :], k2[:].rearrange("p (c t) -> p c t", t=64), -1.0, cev, ALU.mult, ALU.mult),
                             r=[t_k2, t_ce], w=[t_ar])
                        k.op('dve', TT(arT[:, :, 1, :], rb[:].rearrange("p (c t) -> p c t", t=64), wiv, ALU.mult),
                             r=[t_in[0], t_wi], w=[t_ar])
                        k.op('dve', TT(btT[:], a_d[:], wv[:], ALU.mult), r=[t_ad, t_wv], w=[t_btT])
                        k.op('dve', TT(ktT[:], key[:], wv[:], ALU.mult), r=[t_key, t_wv], w=[t_ktT])
                        k.op('dve', STT(prod[:], rb[:], self.cst('r_k')[:, cc:cc + 1], key[:], ALU.mult, ALU.mult),
                             r=[t_in[0], t_key, self.t_cst], w=[t_prod])
                        fns = []
                        for lc in range(NCB):
                            for h in range(2):
                                fns.append(MM(bank[0][HP[h], 256 + lc:256 + lc + 1], prod[HP[h], lc * 64:(lc + 1) * 64],
                                              onesf[HP[h], 0:1], tp=TP[h]))
                        k.op('pe', fns, r=[t_prod, t_lw], w=[bt[0]])
                        k.op('dve', TS(bonus[:, d, c0_:c0_ + NCB], bank[0][:, 256:256 + NCB], 1.0, None, ALU.mult), r=[bt[0]], w=[t_bo])
                        gens = [chunk_group(ln, d, [c0_ + 2 * ln, c0_ + 2 * ln + 1], [2 * ln, 2 * ln + 1]) for ln in range(2)]
                        run_lanes(gens)
                    posof = {ch: p for p, ch in enumerate(order[d])}
                    k.op('dve', lambda e: e.memset(Sall[:, 0, :], 0.0), r=[t_S], w=[t_S])
                    for p in range(NCHUNK):
                        ch = order[d][p]
                        bi = p % 4
                        k.op('pe', [MM(bank[bi][HP[h], 0:64], PcT[HP[h], ch, :], Sall[HP[h], p, :], tp=TP[h]) for h in range(2)],
                             r=[t_P, t_S], w=[bt[bi]])
                        k.op('dve', STT(Sall[:, p + 1, :], bank[bi][:, 0:64], WC[:, ch:ch + 1], Gpp[:, ch, :], ALU.mult, ALU.add),
                             r=[bt[bi], t_WC, t_G], w=[t_S])
                    for g0 in range(0, NCHUNK, 8):
                        g1 = min(g0 + 8, NCHUNK)
                        G = g1 - g0
                        bY = 4 + (g0 // 8) % 4
                        fns = []
                        for ci, ch in enumerate(range(g0, g1)):
                            for h in range(2):
                                fns.append(MM(bank[bY][HP[h], ci * 64:(ci + 1) * 64], RpT[HP[h], ch, :], Sall[HP[h], posof[ch], :], tp=TP[h]))
                        k.op('pe', fns, r=[t_R, t_S], w=[bt[bY]])
                        k.op('dve', TT(ysum[:, g0:g1, :], ysum[:, g0:g1, :],
                                       bank[bY][:, 0:G * 64].rearrange("p (c v) -> p c v", v=64), ALU.add),
                             r=[t_ys, bt[bY]], w=[t_ys])
                for g0 in range(0, NCHUNK, 8):
                    g1 = min(g0 + 8, NCHUNK)
                    G = g1 - g0
                    par = (g0 // 8) % 2
                    bG, bT_ = 0 + par * 2, 1 + par * 2
                    fns = []
                    for ci, ch in enumerate(range(g0, g1)):
                        for h in range(2):
                            hd = 2 * cc + h
                            fns.append(MM(bank[bG][HP[h], ci * 64:(ci + 1) * 64], sgT[:, ch * 64:(ch + 1) * 64],
                                          wsm[:, 2, hd * 64:(hd + 1) * 64], tp=(0, 64 * h)))
                    k.op('pe', fns, r=[t_lw], w=[bt[bG]])
                    yv = yt[:, 0:G, :]
                    pv = lambda b_: bank[b_][:, 0:G * 64].rearrange("p (c v) -> p c v", v=64)
                    k.op('dve', CP(yv, ysum[:, g0:g1, :]), r=[t_ys], w=[t_yt])
                    k.op('dve', lambda e, yv=yv, G=G: e.tensor_reduce(out=st8[:, 0, 0:G], in_=yv, axis=AX.X, op=ALU.add), r=[t_yt], w=[t_st8])
                    k.op('act', ACT(ysq[:, 0:G, :], yv, AF.Square), r=[t_yt], w=[t_ysq])
                    k.op('dve', lambda e, G=G: e.tensor_reduce(out=st8[:, 1, 0:G], in_=ysq[:, 0:G, :], axis=AX.X, op=ALU.add), r=[t_ysq], w=[t_st8])
                    k.op('dve', TS(st8[:, 2, 0:G], st8[:, 0, 0:G], 1.0 / 64, None, ALU.mult), r=[t_st8], w=[t_st8])
                    k.op('dve', TT(st8[:, 3, 0:G], st8[:, 2, 0:G], st8[:, 2, 0:G], ALU.mult), r=[t_st8], w=[t_st8])
                    k.op('dve', STT(st8[:, 4, 0:G], st8[:, 1, 0:G], 1.0 / 64, st8[:, 3, 0:G], ALU.mult, ALU.subtract), r=[t_st8], w=[t_st8])
                    k.op('dve', TS(st8[:, 4, 0:G], st8[:, 4, 0:G], 64e-5, None, ALU.add), r=[t_st8], w=[t_st8])
                    k.op('dve', lambda e, G=G: e.reciprocal(out=st8[:, 4, 0:G], in_=st8[:, 4, 0:G]), r=[t_st8], w=[t_st8])
                    k.op('act', ACT(st8[:, 5, 0:G], st8[:, 4, 0:G], AF.Sqrt), r=[t_st8], w=[t_st8])
                    bc = lambda ap: ap.unsqueeze(2).to_broadcast([128, G, 64])
                    k.op('dve', TT(yv, yv, bc(st8[:, 2, 0:G]), ALU.subtract), r=[t_yt, t_st8], w=[t_yt])
                    k.op('dve', TT(yv, yv, bc(st8[:, 5, 0:G]), ALU.mult), r=[t_yt, t_st8], w=[t_yt])
                    k.op('dve', TT(yv, yv, lnx[:, cc, 0, :].unsqueeze(1).to_broadcast([128, G, 64]), ALU.mult), r=[t_yt, t_lw], w=[t_yt])
                    k.op('dve', TT(yv, yv, lnx[:, cc, 1, :].unsqueeze(1).to_broadcast([128, G, 64]), ALU.add), r=[t_yt, t_lw], w=[t_yt])
                    k.op('dve', TT(st8[:, 0, 0:G], bonus[:, 0, g0:g1], bonus[:, 1, g0:g1], ALU.add), r=[t_bo, t_st8], w=[t_st8])
                    k.op('dve', TT(ysq[:, 0:G, :], Vtok[:, g0:g1, :], bc(st8[:, 0, 0:G]), ALU.mult), r=[t_V, t_st8, t_ysq], w=[t_ysq])
                    k.op('dve', TT(yv, yv, ysq[:, 0:G, :], ALU.add), r=[t_yt, t_ysq], w=[t_yt])
                    k.op('dve', TT(yf[:, 0:G, :], yv, pv(bG), ALU.mult), r=[t_yt, bt[bG]], w=[t_yf])
                    fns = []
                    for ci in range(G):
                        for h in range(2):
                            fns.append(MM(bank[bT_][HP[h], ci * 64:(ci + 1) * 64], yf[HP[h], ci, :], identb[HP[h], h * 64:(h + 1) * 64], tp=TP[h]))
                    k.op('pe', fns, r=[t_yf, self.t_c], w=[bt[bT_]])
                    k.op('act', ACT(yaT[:, 0:G * 64], bank[bT_][:, 0:G * 64], AF.Copy), r=[bt[bT_]], w=[t_ya])
                    k.dma('sp', S['yT'][0, cc * 128:(cc + 1) * 128, g0 * 64:g1 * 64], yaT[:, 0:G * 64], r=[t_ya], w=[self.t_y[0][cc]])
            k.barrier()

    def phase_pool(self, l):
        k, C, I, S = self.k, self.C, self.I, self.S
        bank, bt = self.bank, self.bt
        LP = 8 + CTX + 16 + SEQ + 8
        P0, P1 = 8, 8 + CTX + 16
        with contextlib.ExitStack() as st:
            t_w = Tok()
            pw = k.sb([128, 4, 2, 256], BF16, st)
            corr = k.sb([128, 4, 2, 8], F32, st)
            k.dma('sp', pw[:], S['wb_pool'][l], r=[self.wtok[('pool', l)]], w=[t_w])
            k.dma('sp', corr[:], I['poolcorr'], w=[t_w])
            ub = [k.sb([128, T], BF16, st) for _ in range(2)]
            t_ub = [Tok(), Tok()]
            X = k.sb([128, LP], F32, st)
            Y = k.sb([128, LP], F32, st)
            t_X, t_Y = Tok(), Tok()
            M = k.sb([128, T], F32, st)
            t_M = Tok()
            pT = [k.sb([128, T], BF16, st) for _ in range(2)]
            t_pT = [Tok(), Tok()]
            yo = k.sb([128, T], BF16, st)
            t_yo = Tok()
            k.op('dve', lambda e: e.memset(X[:], 0.0), w=[t_X])
            segs = ((0, CTX, P0), (CTX, T, P1))
            for c in range(8):
                gi = c // 2
                w = 2 << gi
                chn = 27 + c
                b = c % 2
                k.dma('sp', ub[b][:], S['zT'][chn * 128:(chn + 1) * 128, :], r=[self.t_zr[chn]], w=[t_ub[b]])
                for (a, e_, po) in segs:
                    k.op('act', ACT(X[:, po:po + (e_ - a)], ub[b][:, a:e_], AF.Copy), r=[t_ub[b]], w=[t_X])
                cur, tc, nxt, tn = X, t_X, Y, t_Y
                for s_ in range(gi + 1):
                    m = 1 << s_
                    k.op('dve', TT(nxt[:, 0:LP - m], cur[:, 0:LP - m], cur[:, m:LP], ALU.add), r=[tc], w=[tn])
                    cur, tc, nxt, tn = nxt, tn, cur, tc
                for (a, e_, po) in segs:
                    k.op('dve', TS(M[:, a:e_], cur[:, po - w // 2:po - w // 2 + (e_ - a)], 1.0 / w, None, ALU.mult), r=[tc], w=[t_M])
                    k.op('dve', TT(M[:, a:a + 8], M[:, a:a + 8], corr[:, gi, 0, :], ALU.mult), r=[t_M, t_w], w=[t_M])
                    k.op('dve', TT(M[:, e_ - 8:e_], M[:, e_ - 8:e_], corr[:, gi, 1, :], ALU.mult), r=[t_M, t_w], w=[t_M])
                k.op('dve', TT(pT[b][:], M[:], ub[b][:], ALU.subtract), r=[t_M, t_ub[b]], w=[t_pT[b]])
                for (z0, z1) in ((0, P0), (P0 + CTX, P1), (P1 + SEQ, LP)):
                    k.op('dve', lambda e, z0=z0, z1=z1: e.memset(X[:, z0:z1], 0.0), r=[t_X], w=[t_X])
                if c % 2 == 1:
                    for dd in range(2):
                        och = gi * 2 + dd
                        ps_ = self.cst('pool_scale')[:, och:och + 1]
                        for ti, (t0, NT, s) in enumerate(self.tiles()):
                            bi = ti % 4
                            k.op('pe', [MM(bank[bi][:, 0:NT], pw[:, gi, c2, dd * 128:(dd + 1) * 128], pT[c2][:, t0:t0 + NT],
                                           start=(c2 == 0), stop=(c2 == 1)) for c2 in range(2)],
                                 r=[t_w, t_pT[0], t_pT[1]], w=[bt[bi]])
                            k.alt(ACT(yo[:, t0:t0 + NT], bank[bi][:, 0:NT], AF.Copy, scale=ps_),
                                  TS(yo[:, t0:t0 + NT], bank[bi][:, 0:NT], ps_, None, ALU.mult),
                                  r=[bt[bi], self.t_cst], w=[t_yo])
                        k.dma('sp', S['yT'][1, och * 128:(och + 1) * 128, :], yo[:], r=[t_yo], w=[self.t_y[1][och]])
            k.barrier()

    def phase_attn(self, l, ctx_out):
        k, C, I, S = self.k, self.C, self.I, self.S
        bank, bt = self.bank, self.bt
        identb = C['identb']
        lam_init = 0.8 - 0.6 * math.exp(-0.3 * l)
        NKC = T // 128
        with contextlib.ExitStack() as st:
            t_w = Tok()
            cosT = k.sb([128, SEQ], F32, st)
            sinT = k.sb([128, SEQ], F32, st)
            stage = k.sb([128, 256], F32, st)
            perm = k.sb([128, 128], BF16, st)
            g2 = k.sb([128, 128], F32, st)
            lamt = k.sb([128, 8], F32, st)
            zb = k.sb([128, 512], BF16, st)
            k.dma('sp', cosT[:], I['ropecos'], w=[t_w])
            k.dma('sp', sinT[:], I['ropesin'], w=[t_w])
            k.dma('sp', stage[:, 0:128], I['ropeperm'], w=[t_w])
            k.op('dve', CP(perm[:], stage[:, 0:128]), r=[t_w], w=[t_w])
            k.dma('sp', g2[:], I['subln'][l], w=[t_w])
            k.op('dve', TS(g2[:], g2[:], 1.0 - lam_init, None, ALU.mult), r=[t_w], w=[t_w])
            k.dma('sp', stage[:], I['lamqk'][l], w=[t_w])
            k.op('dve', TT(stage[:, 0:64], stage[:, 0:64], stage[:, 64:128], ALU.mult), r=[t_w], w=[t_w])
            k.op('dve', TT(stage[:, 128:192], stage[:, 128:192], stage[:, 192:256], ALU.mult), r=[t_w], w=[t_w])
            k.op('dve', lambda e: e.tensor_reduce(out=lamt[:, 0:1], in_=stage[:, 0:64], axis=AX.X, op=ALU.add), r=[t_w], w=[t_w])
            k.op('dve', lambda e: e.tensor_reduce(out=lamt[:, 1:2], in_=stage[:, 128:192], axis=AX.X, op=ALU.add), r=[t_w], w=[t_w])
            k.op('act', ACT(lamt[:, 2:4], lamt[:, 0:2], AF.Exp), r=[t_w], w=[t_w])
            k.op('dve', TT(lamt[:, 4:5], lamt[:, 3:4], lamt[:, 2:3], ALU.subtract), r=[t_w], w=[t_w])
            k.op('dve', TS(lamt[:, 5:6], lamt[:, 4:5], -lam_init, None, ALU.add), r=[t_w], w=[t_w])
            k.op('dve', lambda e: e.memset(zb[:], 0.0), w=[t_w])
            neg_lam = lamt[:, 5:6]
            qT, kT, vT = (k.sb([128, T], BF16, st) for _ in range(3))
            t_q, t_k, t_v = Tok(), Tok(), Tok()
            Vaug = k.sb([128, NKC, 130], BF16, st)
            t_Va = Tok()
            k.op('dve', lambda e: e.memset(Vaug[:, :, 128:130], 1.0), w=[t_Va])
            t1 = [k.sb([128, 512], F32, st) for _ in range(2)]
            t2 = [k.sb([128, 512], F32, st) for _ in range(2)]
            t_t1, t_t2 = [Tok(), Tok()], [Tok(), Tok()]
            pT = [k.sb([128, 512], BF16, st) for _ in range(3)]
            t_p = [Tok() for _ in range(3)]
            rc = k.sb([128, 16], F32, st)
            t_rc = Tok()
            o32 = [k.sb([128, 128], F32, st) for _ in range(2)]
            t_o = [Tok(), Tok()]
            junk = k.sb([128, 128], F32, st)
            t_junk = Tok()
            on = k.sb([128, 4, 128], BF16, st)
            t_on = Tok()
            yc = [k.sb([128, 512], BF16, st) for _ in range(2)]
            t_yc = [Tok(), Tok()]
            cnt = dict(rope=0, p=0, sc=0, o=0, yc=0)
            for h in range(8):
                for (dst, tk, base) in ((qT, t_q, 35), (kT, t_k, 43), (vT, t_v, 51)):
                    chn = base + h
                    k.dma('sp', dst[:], S['zT'][chn * 128:(chn + 1) * 128, :], r=[self.t_zr[chn]], w=[tk])
                for (buf, tk) in ((qT, t_q), (kT, t_k)):
                    for ti in range(SEQ // 512):
                        cs = slice(CTX + ti * 512, CTX + (ti + 1) * 512)
                        ts_ = slice(ti * 512, (ti + 1) * 512)
                        i_ = cnt['rope'] % 2
                        cnt['rope'] += 1
                        bi = i_
                        k.op('pe', MM(bank[bi][:, 0:512], perm[:], buf[:, cs]), r=[tk, t_w], w=[bt[bi]])
                        k.op('pool', TT(t1[i_][:], buf[:, cs], cosT[:, ts_], ALU.mult), r=[tk, t_w], w=[t_t1[i_]])
                        k.op('dve', TT(t2[i_][:], bank[bi][:, 0:512], sinT[:, ts_], ALU.mult), r=[bt[bi], t_w], w=[t_t2[i_]])
                        k.op('dve', TT(buf[:, cs], t1[i_][:], t2[i_][:], ALU.add), r=[t_t1[i_], t_t2[i_]], w=[tk])
                for g0 in range(0, NKC, 8):
                    g1 = min(g0 + 8, NKC)
                    bi = 2 + (g0 // 8) % 2
                    pb = bank[bi][:].bitcast(BF16)
                    k.op('pe', [lambda e, kc=kc, pb=pb, g0=g0: e.transpose(out=pb[:, (kc - g0) * 128:(kc - g0 + 1) * 128],
                                                                        in_=vT[:, kc * 128:(kc + 1) * 128], identity=identb[:])
                                for kc in range(g0, g1)], r=[t_v, self.t_c], w=[bt[bi]])
                    k.op('act', ACT(Vaug[:, g0:g1, 0:128], pb[:, 0:(g1 - g0) * 128].rearrange("p (c e) -> p c e", e=128), AF.Copy),
                         r=[bt[bi]], w=[t_Va])
                qtiles = [(CTX + i * 512, 512, list(range(NKC))) for i in range(SEQ // 512)]
                if ctx_out:
                    qtiles = [(0, CTX, [0, 1])] + qtiles
                for (q0, NQ, kcs) in qtiles:
                    nqs = NQ // 128
                    accb = (5, 6, 7)

                    def acc_ap(j, qs):
                        idx = j * 4 + qs
                        return bank[accb[idx // 3]][:, (idx % 3) * 132:(idx % 3) * 132 + 129]
                    for b_ in accb:
                        k.op('pe', MM(bank[b_][:, 0:512], zb[:, 0:128], zb[:, 0:512]), r=[t_w], w=[bt[b_]])
                    for j in range(2):
                        jp = slice(j * 64, (j + 1) * 64)
                        for kc in kcs:
                            bi = cnt['sc'] % 4
                            cnt['sc'] += 1
                            pi = cnt['p'] % 3
                            cnt['p'] += 1
                            k.op('pe', MM(bank[bi][:, 0:NQ], kT[jp, kc * 128:(kc + 1) * 128], qT[jp, q0:q0 + NQ], tp=(64 * j, 0)),
                                 r=[t_q, t_k], w=[bt[bi]])
                            k.op('act', ACT(pT[pi][:, 0:NQ], bank[bi][:, 0:NQ], AF.Exp, scale=0.125), r=[bt[bi]], w=[t_p[pi]])
                            k.op('pe', [MM(acc_ap(j, qs), pT[pi][:, qs * 128:(qs + 1) * 128], Vaug[:, kc, 0:129], start=False, stop=True)
                                        for qs in range(nqs)],
                                 r=[t_p[pi], t_Va], w=[bt[b_] for b_ in accb])
                    accr = [bt[b_] for b_ in accb]
                    for j in range(2):
                        for qs in range(nqs):
                            k.op('dve', lambda e, j=j, qs=qs: e.reciprocal(out=rc[:, j * 4 + qs:j * 4 + qs + 1], in_=acc_ap(j, qs)[:, 128:129]),
                                 r=accr, w=[t_rc])
                    k.op('dve', TS(rc[:, 8:8 + nqs], rc[:, 4:4 + nqs], neg_lam, None, ALU.mult), r=[t_rc, t_w], w=[t_rc])
                    for qs in range(nqs):
                        oi = cnt['o'] % 2
                        cnt['o'] += 1
                        o_ = o32[oi]
                        k.op('dve', TS(o_[:], acc_ap(0, qs)[:, 0:128], rc[:, qs:qs + 1], None, ALU.mult), r=accr + [t_rc], w=[t_o[oi]])
                        k.op('dve', STT(o_[:], acc_ap(1, qs)[:, 0:128], rc[:, 8 + qs:9 + qs], o_[:], ALU.mult, ALU.add),
                             r=accr + [t_rc, t_o[oi]], w=[t_o[oi]])
                        k.op('act', ACT(junk[:], o_[:], AF.Square, accum_out=rc[:, 12 + qs:13 + qs]), r=[t_o[oi]], w=[t_junk, t_rc])
                        k.op('dve', TS(rc[:, 12 + qs:13 + qs], rc[:, 12 + qs:13 + qs], 1.0 / 128, 1e-5, ALU.mult, ALU.add), r=[t_rc], w=[t_rc])
                        k.op('dve', lambda e, qs=qs: e.reciprocal(out=rc[:, 12 + qs:13 + qs], in_=rc[:, 12 + qs:13 + qs]), r=[t_rc], w=[t_rc])
                        k.op('act', ACT(rc[:, 12 + qs:13 + qs], rc[:, 12 + qs:13 + qs], AF.Sqrt), r=[t_rc], w=[t_rc])
                        k.op('dve', STT(on[:, qs, :], o_[:], rc[:, 12 + qs:13 + qs], g2[:], ALU.mult, ALU.mult), r=[t_o[oi], t_rc, t_w], w=[t_on])
                    bi = 4
                    pb = bank[bi][:].bitcast(BF16)
                    k.op('pe', [lambda e, qs=qs, pb=pb: e.transpose(out=pb[:, qs * 128:(qs + 1) * 128], in_=on[:, qs, :], identity=identb[:])
                                for qs in range(nqs)], r=[t_on, self.t_c], w=[bt[bi]])
                    yi = cnt['yc'] % 2
                    cnt['yc'] += 1
                    k.op('act', ACT(yc[yi][:, 0:NQ], pb[:, 0:NQ], AF.Copy), r=[bt[bi]], w=[t_yc[yi]])
                    k.dma('sp', S['yT'][2, h * 128:(h + 1) * 128, q0:q0 + NQ], yc[yi][:, 0:NQ], r=[t_yc[yi]], w=[self.t_y[2][h]])
            k.barrier()

    @staticmethod
    def tiles256():
        return [(i * 256, 256, 1 if i == 0 else 0) for i in range(T // 256)]

    def phase_merge(self, l, ctx_out):
        k, C, I, S = self.k, self.C, self.I, self.S
        bank, bt = self.bank, self.bt
        NT = 256
        with contextlib.ExitStack() as st:
            gbc = k.sb([128, 2, D], F32, st)
            t_g = Tok()
            k.dma('sp', gbc[:], S['gbc'][0].rearrange("s p d -> p s d"), r=[self.t_gbc], w=[t_g])
            xt = k.sb([128, 2, D], F32, st)
            t_xt = Tok()
            yi = k.sb([128, 3, 8, NT], BF16, st)
            t_yi = Tok()
            gz = [k.sb([128, 3, 4, NT], BF16, st) for _ in range(2)]
            t_gz = [Tok(), Tok()]
            gs = [k.sb([128, NT], F32, st) for _ in range(3)]
            t_gs = [Tok() for _ in range(3)]
            acc = [k.sb([128, NT], F32, st) for _ in range(2)]
            t_acc = [Tok(), Tok()]
            accT = k.sb([128, 16, NT], BF16, st)
            t_accT = Tok()
            wbr = [k.sb([128, 3, 8, 512], BF16, st) for _ in range(2)]
            t_wbr = [Tok(), Tok()]
            wo = [k.sb([128, 16, 512], BF16, st) for _ in range(2)]
            t_wo = [Tok(), Tok()]
            tmp = [k.sb([128, 512], F32, st) for _ in range(2)]
            t_tmp = [Tok(), Tok()]
            n = dict(br=0, wo=0, gz=0, tmp=0, acc=0)
            for (t0, _, s) in self.tiles256():
                if s == 1 and not ctx_out:
                    continue
                k.dma('sp', xt[:], S['xres'][t0:t0 + NT, :].rearrange("(s p) d -> p s d", p=128), r=[self.t_x], w=[t_xt])
                for i in range(3):
                    k.dma('sp', yi[:, i], S['yT'][i][:, t0:t0 + NT].rearrange("(c p) t -> p c t", p=128),
                          r=self.t_y[i], w=[t_yi])
                for s4 in range(4):
                    wb_ = n['br'] % 2
                    n['br'] += 1
                    for i in range(3):
                        k.dma('sp', wbr[wb_][:, i], S['wb_br'][l, i, s4], r=[self.wtok[('br', l, i, s4)]], w=[t_wbr[wb_]])
                    gb_ = n['gz'] % 2
                    n['gz'] += 1
                    for i in range(3):
                        r0 = 59 + i * 16 + s4 * 4
                        k.dma('sp', gz[gb_][:, i], S['zT'][r0 * 128:(r0 + 4) * 128, t0:t0 + NT].rearrange("(m p) t -> p m t", p=128),
                              r=[self.t_zr[r0 + j_] for j_ in range(4)], w=[t_gz[gb_]])
                    for mi in range(4):
                        m = s4 * 4 + mi
                        ai = n['acc'] % 2
                        n['acc'] += 1
                        for i in range(3):
                            bi = i
                            k.op('pe', [MM(bank[bi][:, 0:NT], wbr[wb_][:, i, kc, mi * 128:(mi + 1) * 128], yi[:, i, kc, :],
                                           start=(kc == 0), stop=(kc == 7)) for kc in range(8)],
                                 r=[t_wbr[wb_], t_yi], w=[bt[bi]])
                            k.op('act', ACT(gs[i][:], gz[gb_][:, i, mi, :], AF.Sigmoid), r=[t_gz[gb_]], w=[t_gs[i]])
                        k.op('dve', TT(acc[ai][:], bank[0][:, 0:NT], gs[0][:], ALU.mult), r=[bt[0], t_gs[0]], w=[t_acc[ai]])
                        k.op('dve', TT(gs[1][:], bank[1][:, 0:NT], gs[1][:], ALU.mult), r=[bt[1], t_gs[1]], w=[t_gs[1]])
                        k.op('dve', TT(gs[2][:], bank[2][:, 0:NT], gs[2][:], ALU.mult), r=[bt[2], t_gs[2]], w=[t_gs[2]])
                        k.op('pool', TT(acc[ai][:], acc[ai][:], gs[1][:], ALU.add), r=[t_acc[ai], t_gs[1]], w=[t_acc[ai]])
                        k.op('pool', TT(accT[:, m, :], acc[ai][:], gs[2][:], ALU.add), r=[t_acc[ai], t_gs[2]], w=[t_accT])
                for cg in range(4):
                    wb_ = n['wo'] % 2
                    n['wo'] += 1
                    k.dma('sp', wo[wb_][:], S['wb_out'][l, cg], r=[self.wtok[('out', l, cg)]], w=[t_wo[wb_]])
                    for sub in range(2):
                        bi = 4 + (cg * 2 + sub) % 4
                        k.op('pe', [MM(bank[bi][:, 0:512], accT[:, kc, sub * 128:(sub + 1) * 128], wo[wb_][:, kc, :],
                                       start=(kc == 0), stop=(kc == 15)) for kc in range(16)],
                             r=[t_accT, t_wo[wb_]], w=[bt[bi]])
                        ti = n['tmp'] % 2
                        n['tmp'] += 1
                        cs = slice(cg * 512, (cg + 1) * 512)
                        k.op('dve', TT(tmp[ti][:], bank[bi][:, 0:512], gbc[:, s, cs], ALU.mult), r=[bt[bi], t_g], w=[t_tmp[ti]])
                        k.op('pool', TT(xt[:, sub, cs], xt[:, sub, cs], tmp[ti][:], ALU.add), r=[t_tmp[ti], t_xt], w=[t_xt])
                k.dma('sp', S['xres'][t0:t0 + NT, :].rearrange("(s p) d -> p s d", p=128), xt[:], r=[t_xt], w=[self.t_x])
            k.barrier()

    def phase_ffn(self, l, ctx_out):
        k, C, I, S = self.k, self.C, self.I, self.S
        bank, bt = self.bank, self.bt
        NT = 256
        with contextlib.ExitStack() as st:
            gbc = k.sb([128, 2, D], F32, st)
            t_g = Tok()
            k.dma('sp', gbc[:], S['gbc'][1].rearrange("s p d -> p s d"), r=[self.t_gbc], w=[t_g])
            xt = k.sb([128, 2, D], F32, st)
            t_xt = Tok()
            hT = k.sb([128, 16, NT], BF16, st)
            t_hT = Tok()
            scr = self.norm_scratch(st, 2)
            actT = k.sb([128, 44, NT], BF16, st)
            t_act = Tok()
            w1 = [k.sb([128, 16, 512], BF16, st) for _ in range(4)]
            t_w1 = [Tok() for _ in range(4)]
            w2 = [k.sb([128, 44, 256], BF16, st) for _ in range(2)]
            t_w2 = [Tok(), Tok()]
            sg = [k.sb([128, NT], F32, st) for _ in range(2)]
            t_sg = [Tok(), Tok()]
            tmp = [k.sb([128, 256], F32, st) for _ in range(2)]
            t_tmp = [Tok(), Tok()]
            n = dict(w1=0, w2=0, sg=0, tmp=0, pb=0)
            for (t0, _, s) in self.tiles256():
                if s == 1 and not ctx_out:
                    continue
                k.dma('sp', xt[:], S['xres'][t0:t0 + NT, :].rearrange("(s p) d -> p s d", p=128), r=[self.t_x], w=[t_xt])
                self.norm_transpose(xt, t_xt, 2, 1, s, hT, t_hT, scr)
                for jj in range(11):
                    bufs = []
                    for half in range(2):
                        wb_ = n['w1'] % 4
                        n['w1'] += 1
                        sl = jj + 11 * half
                        k.dma('sp', w1[wb_][:], S['wb_f1'][l, sl], r=[self.wtok[('f1', l, sl)]], w=[t_w1[wb_]])
                        bufs.append(wb_)
                    for mi in range(4):
                        j = jj * 4 + mi
                        pb = n['pb'] % 2
                        n['pb'] += 1
                        bg, bu = 0 + pb, 2 + pb
                        for (bi, wb_) in ((bg, bufs[0]), (bu, bufs[1])):
                            k.op('pe', [MM(bank[bi][:, 0:NT], w1[wb_][:, kc, mi * 128:(mi + 1) * 128], hT[:, kc, 0:NT],
                                           start=(kc == 0), stop=(kc == 15)) for kc in range(16)],
                                 r=[t_w1[wb_], t_hT], w=[bt[bi]])
                        si = n['sg'] % 2
                        n['sg'] += 1
                        k.op('act', ACT(sg[si][:], bank[bg][:, 0:NT], AF.Silu), r=[bt[bg]], w=[t_sg[si]])
                        k.op('dve', TT(actT[:, j, :], bank[bu][:, 0:NT], sg[si][:], ALU.mult), r=[bt[bu], t_sg[si]], w=[t_act])
                for cg in range(8):
                    wb_ = n['w2'] % 2
                    n['w2'] += 1
                    k.dma('sp', w2[wb_][:], S['wb_f2'][l, cg], r=[self.wtok[('f2', l, cg)]], w=[t_w2[wb_]])
                    for sub in range(2):
                        bi = 4 + (cg * 2 + sub) % 4
                        k.op('pe', [MM(bank[bi][:, 0:256], actT[:, j, sub * 128:(sub + 1) * 128], w2[wb_][:, j, :],
                                       start=(j == 0), stop=(j == 43)) for j in range(44)],
                             r=[t_act, t_w2[wb_]], w=[bt[bi]])
                        ti = n['tmp'] % 2
                        n['tmp'] += 1
                        cs = slice(cg * 256, (cg + 1) * 256)
                        k.op('dve', TT(tmp[ti][:], bank[bi][:, 0:256], gbc[:, s, cs], ALU.mult), r=[bt[bi], t_g], w=[t_tmp[ti]])
                        k.op('pool', TT(xt[:, sub, cs], xt[:, sub, cs], tmp[ti][:], ALU.add), r=[t_tmp[ti], t_xt], w=[t_xt])
                k.dma('sp', S['xres'][t0:t0 + NT, :].rearrange("(s p) d -> p s d", p=128), xt[:], r=[t_xt], w=[self.t_x])
            k.barrier()

    def final(self):
        k, C, I, S = self.k, self.C, self.I, self.S
        with contextlib.ExitStack() as st:
            fg = k.sb([128, D], F32, st)
            t_fg = Tok()
            k.dma('sp', fg[:], I['final_g'], w=[t_fg])
            xt = [k.sb([128, 4, D], F32, st) for _ in range(2)]
            t_xt = [Tok(), Tok()]
            ot = [k.sb([128, 4, D], F32, st) for _ in range(2)]
            t_ot = [Tok(), Tok()]
            junk = k.sb([128, D], BF16, st)
            ssq = k.sb([128, 8], F32, st)
            t_j, t_s = Tok(), Tok()
            for i in range(SEQ // 512):
                b = i % 2
                t0 = CTX + i * 512
                k.dma('sp', xt[b][:], S['xres'][t0:t0 + 512, :].rearrange("(s p) d -> p s d", p=128), r=[self.t_x], w=[t_xt[b]])
                for sub in range(4):
                    k.op('act', ACT(junk[:], xt[b][:, sub, :], AF.Square, accum_out=ssq[:, sub:sub + 1]), r=[t_xt[b]], w=[t_j, t_s])
                k.op('dve', TS(ssq[:, 4:8], ssq[:, 0:4], 1.0 / D, 1e-6, ALU.mult, ALU.add), r=[t_s], w=[t_s])
                k.op('dve', lambda e: e.reciprocal(out=ssq[:, 4:8], in_=ssq[:, 4:8]), r=[t_s], w=[t_s])
                k.op('act', ACT(ssq[:, 4:8], ssq[:, 4:8], AF.Sqrt), r=[t_s], w=[t_s])
                for sub in range(4):
                    k.op('dve', STT(ot[b][:, sub, :], xt[b][:, sub, :], ssq[:, 4 + sub:5 + sub], fg[:], ALU.mult, ALU.mult),
                         r=[t_xt[b], t_s, t_fg], w=[t_ot[b]])
                k.dma('sp', self.out[i * 512:(i + 1) * 512, :].rearrange("(s p) d -> p s d", p=128), ot[b][:], r=[t_ot[b]])


_PROG = {}


def make_inputs(inp, b, consts, lc):
    cst, lnx, sub, lam = lc
    cvec = np.stack([fT(inp['c'][b], 16), fT(inp['c_ctx'], 16)], axis=2)
    m = dict(x=np.ascontiguousarray(inp['x'][b]), ctx=np.ascontiguousarray(inp['ctx'][b]), cvec=np.ascontiguousarray(cvec),
             w_ada=inp['w_ada'], w_in=inp['w_in'], w_branch=inp['w_branch'], w_out=inp['w_out'],
             w_ffn_in=inp['w_ffn_in'], w_ffn_out=inp['w_ffn_out'], pool_w=inp['pool_w'],
             decay_up=inp['decay_up'].reshape(DEPTH, 128, BW), iclr_up=inp['iclr_up'].reshape(DEPTH, 128, BW),
             gate_up=inp['gate_up'], cst=cst, lnx=lnx, subln=sub, lamqk=lam,
             final_g=np.broadcast_to(np.asarray(inp['final_g'], np.float32)[None, :], (128, D)).copy())
    ba = np.asarray(inp['b_ada'], np.float32)
    m['bada_bc'] = np.ascontiguousarray(np.broadcast_to(np.stack([ba[:, 2 * D:3 * D], ba[:, 5 * D:6 * D]], 1)[:, :, None, :], (ba.shape[0], 2, 128, D)))
    m.update(consts)
    return m


def kernel(**inputs):
    inp = {k_: np.asarray(v) for k_, v in inputs.items()}
    if 'p' not in _PROG:
        _PROG['p'] = Prog()
    prog = _PROG['p']
    consts = host_consts()
    lc = host_layer_consts(inp)
    in_maps = [make_inputs(inp, b, consts, lc) for b in range(8)]
    res = run_bass_kernel_spmd(prog.nc, in_maps, core_ids=list(range(8)))
    return np.stack([r['out'] for r in res.results], axis=0).astype(np.float32)
```
